# Optimizing a Trainium2 kernel written in Bass

```python
import math
import jax, jax.numpy as jnp
from jax import lax
import numpy as np

D_MODEL = 1024
BATCH = 16
SEQ = 4096
DEPTH = 1
DEC_BATCH = 32
DEC_SEQ = 64
PAST_LEN = 4096

CHUNK = 64
Q_BLOCK = 128
HEAD_DIM = 64
N_HEADS_A = 8
N_KV_A = 2
N_IDX_HEADS = 8
IDX_DIM = 64
TOPK_MAX = 256
N_HEADS_B = 8
MIX_WIDTH = (N_HEADS_A + N_HEADS_B) * HEAD_DIM
ROT_DIM = HEAD_DIM // 4
ROPE_THETA = 500000.0
D_FF = 2816
LN_EPS = 1e-5
ALPHA = (2.0 * DEPTH) ** 0.25
BETA = (8.0 * DEPTH) ** -0.25
COL_SIZES = (N_HEADS_A * HEAD_DIM, N_KV_A * HEAD_DIM, N_KV_A * HEAD_DIM,
             N_IDX_HEADS * IDX_DIM, IDX_DIM, N_IDX_HEADS,
             N_HEADS_B * HEAD_DIM, N_HEADS_B * HEAD_DIM, N_HEADS_B * HEAD_DIM, N_HEADS_B)
N_IN = sum(COL_SIZES)

kernel_name = "hybrid_dsa_fox_streaming_step"


def layer_norm(x, g, b):
    xf = x.astype(jnp.float32)
    mu = jnp.mean(xf, axis=-1, keepdims=True)
    var = jnp.mean(jnp.square(xf - mu), axis=-1, keepdims=True)
    return ((xf - mu) * lax.rsqrt(var + LN_EPS) * g.astype(jnp.float32) + b.astype(jnp.float32)).astype(x.dtype)


def swiglu(x, w_gate, w_up, w_down):
    return (jax.nn.silu(x @ w_gate) * (x @ w_up)) @ w_down


def macaron_half(x, g, b, w_gate, w_up, w_down):
    return layer_norm(ALPHA * x + 0.5 * swiglu(x, w_gate, w_up, w_down), g, b)


def partial_rope(x, pos):
    half = ROT_DIM // 2
    inv_freq = ROPE_THETA ** (-jnp.arange(half, dtype=jnp.float32) * 2.0 / ROT_DIM)
    ang = pos.astype(jnp.float32)[:, None] * inv_freq[None, :]
    cos = jnp.cos(ang)[None, :, None, :].astype(x.dtype)
    sin = jnp.sin(ang)[None, :, None, :].astype(x.dtype)
    x1 = x[..., :half]
    x2 = x[..., half:ROT_DIM]
    return jnp.concatenate([x1 * cos - x2 * sin, x2 * cos + x1 * sin, x[..., ROT_DIM:]], axis=-1)


def project(h, pos, w_in, b_f):
    B, T = h.shape[0], h.shape[1]
    z = h @ w_in
    parts = []
    off = 0
    for s in COL_SIZES:
        parts.append(z[..., off:off + s])
        off += s
    q_a, k_a, v_a, q_i, k_i, w_i, q_b, k_b, v_b, f_b = parts
    q_a = partial_rope(q_a.reshape(B, T, N_HEADS_A, HEAD_DIM), pos)
    k_a = partial_rope(k_a.reshape(B, T, N_KV_A, HEAD_DIM), pos)
    v_a = v_a.reshape(B, T, N_KV_A, HEAD_DIM)
    q_i = partial_rope(q_i.reshape(B, T, N_IDX_HEADS, IDX_DIM), pos)
    k_i = partial_rope(k_i.reshape(B, T, 1, IDX_DIM), pos)[:, :, 0]
    w_i = w_i * (N_IDX_HEADS ** -0.5)
    q_b = q_b.reshape(B, T, N_HEADS_B, HEAD_DIM)
    k_b = k_b.reshape(B, T, N_HEADS_B, HEAD_DIM)
    v_b = v_b.reshape(B, T, N_HEADS_B, HEAD_DIM)
    logf_b = jax.nn.log_sigmoid((f_b + b_f).astype(jnp.float32))
    return q_a, k_a, v_a, q_i, k_i, w_i, q_b, k_b, v_b, logf_b


def dsa_block(q, qi, wi, qpos, k, v, ki, k_sel):
    B, Tq = q.shape[0], q.shape[1]
    S = k.shape[1]
    rel = jax.nn.relu(jnp.einsum('bthi,bsi->bths', qi, ki) * (IDX_DIM ** -0.5))
    score = jnp.einsum('bth,bths->bts', wi, rel).astype(jnp.float32)
    qchunk = qpos // CHUNK
    admissible = (jnp.arange(S, dtype=jnp.int32) // CHUNK)[None, :] <= qchunk[:, None]
    score = jnp.where(admissible[None], score, -jnp.inf)
    _, idx = lax.top_k(score, k_sel)
    valid = (idx // CHUNK) <= qchunk[None, :, None]
    kg = jax.vmap(lambda kk, ii: kk[ii])(k, idx)
    vg = jax.vmap(lambda vv, ii: vv[ii])(v, idx)
    qg = q.reshape(B, Tq, N_KV_A, N_HEADS_A // N_KV_A, HEAD_DIM)
    logits = jnp.einsum('bthgd,btnhd->bthgn', qg, kg).astype(jnp.float32) * (HEAD_DIM ** -0.5)
    logits = jnp.where(valid[:, :, None, None, :], logits, -jnp.inf)
    p = jax.nn.softmax(logits, axis=-1).astype(v.dtype)
    o = jnp.einsum('bthgn,btnhd->bthgd', p, vg)
    return o.reshape(B, Tq, N_HEADS_A * HEAD_DIM)


def fox_block(q, cq, qpos, k, v, ck_t):
    B, Tq = q.shape[0], q.shape[1]
    S = k.shape[1]
    logits = jnp.einsum('bthd,bshd->bhts', q, k).astype(jnp.float32) * (HEAD_DIM ** -0.5)
    bias = jnp.transpose(cq, (0, 2, 1))[..., :, None] - ck_t[..., None, :]
    causal = jnp.arange(S, dtype=jnp.int32)[None, :] <= qpos[:, None]
    logits = jnp.where(causal, logits + bias, -jnp.inf)
    p = jax.nn.softmax(logits, axis=-1).astype(v.dtype)
    o = jnp.einsum('bhts,bshd->bthd', p, v)
    return o.reshape(B, Tq, N_HEADS_B * HEAD_DIM)


def mixer_prompt(q_a, k_a, v_a, q_i, k_i, w_i, q_b, k_b, v_b, logf_b):
    B, T = q_a.shape[0], q_a.shape[1]
    nb = T // Q_BLOCK
    k_sel = min(TOPK_MAX, T // 4)
    pos = jnp.arange(T, dtype=jnp.int32)
    cum = jnp.cumsum(logf_b, axis=1)
    ck_t = jnp.transpose(cum, (0, 2, 1))

    def blocks(a):
        return jnp.moveaxis(a.reshape((B, nb, Q_BLOCK) + a.shape[2:]), 1, 0)

    def body(xs):
        qa, qi, wi, qb, cq, qpos = xs
        oa = dsa_block(qa, qi, wi, qpos, k_a, v_a, k_i, k_sel)
        ob = fox_block(qb, cq, qpos, k_b, v_b, ck_t)
        return jnp.concatenate([oa, ob], axis=-1)

    out = lax.map(body, (blocks(q_a), blocks(q_i), blocks(w_i), blocks(q_b), blocks(cum),
                         pos.reshape(nb, Q_BLOCK)))
    return jnp.moveaxis(out, 0, 1).reshape(B, T, MIX_WIDTH)


def mixer_sample(q_a, k_a, v_a, q_i, k_i, w_i, q_b, k_b, v_b, logf_b, pos,
                 c_k_a, c_v_a, c_kidx, c_k_b, c_v_b, c_logf):
    past = c_k_a.shape[1]
    ka_all = jnp.concatenate([c_k_a, k_a], axis=1)
    va_all = jnp.concatenate([c_v_a, v_a], axis=1)
    ki_all = jnp.concatenate([c_kidx, k_i], axis=1)
    kb_all = jnp.concatenate([c_k_b, k_b], axis=1)
    vb_all = jnp.concatenate([c_v_b, v_b], axis=1)
    L = ka_all.shape[1]
    k_sel = min(TOPK_MAX, L // 4)
    cum = jnp.cumsum(jnp.concatenate([c_logf.astype(jnp.float32), logf_b], axis=1), axis=1)
    oa = dsa_block(q_a, q_i, w_i, pos, ka_all, va_all, ki_all, k_sel)
    ob = fox_block(q_b, cum[:, past:], pos, kb_all, vb_all, jnp.transpose(cum, (0, 2, 1)))
    return jnp.concatenate([oa, ob], axis=-1)


def setup_inputs(seed: int = 0) -> dict:
    key = jax.random.key(seed)
    ks = jax.random.split(key, 24)
    f32 = jnp.float32
    nrm = lambda k, shape, scale: jax.random.normal(k, shape, f32) * scale
    cache_rows = (DEPTH, DEC_BATCH, PAST_LEN)
    return {
        "x_prompt": nrm(ks[0], (BATCH, SEQ, D_MODEL), 1.0),
        "x_sample": nrm(ks[1], (DEC_BATCH, DEC_SEQ, D_MODEL), 1.0),
        "cache_k_a": nrm(ks[2], cache_rows + (N_KV_A, HEAD_DIM), 1.0),
        "cache_v_a": nrm(ks[3], cache_rows + (N_KV_A, HEAD_DIM), 1.0),
        "cache_kidx_a": nrm(ks[4], cache_rows + (IDX_DIM,), 1.0),
        "cache_k_b": nrm(ks[5], cache_rows + (N_HEADS_B, HEAD_DIM), 1.0),
        "cache_v_b": nrm(ks[6], cache_rows + (N_HEADS_B, HEAD_DIM), 1.0),
        "cache_logf_b": jax.nn.log_sigmoid(jax.random.uniform(ks[7], cache_rows + (N_HEADS_B,), f32, 1.0, 5.0)
                                           + nrm(ks[8], cache_rows + (N_HEADS_B,), 1.0)),
        "w_in": nrm(ks[9], (DEPTH, D_MODEL, N_IN), D_MODEL ** -0.5),
        "b_f": jax.random.uniform(ks[10], (DEPTH, N_HEADS_B), f32, 1.0, 5.0),
        "w_out": nrm(ks[11], (DEPTH, MIX_WIDTH, D_MODEL), BETA * MIX_WIDTH ** -0.5),
        "ln1_g": 1.0 + nrm(ks[12], (DEPTH, D_MODEL), 0.05),
        "ln1_b": nrm(ks[13], (DEPTH, D_MODEL), 0.02),
        "ffn1_w_gate": nrm(ks[14], (DEPTH, D_MODEL, D_FF), D_MODEL ** -0.5),
        "ffn1_w_up": nrm(ks[15], (DEPTH, D_MODEL, D_FF), D_MODEL ** -0.5),
        "ffn1_w_down": nrm(ks[16], (DEPTH, D_FF, D_MODEL), BETA * D_FF ** -0.5),
        "ln2_g": 1.0 + nrm(ks[17], (DEPTH, D_MODEL), 0.05),
        "ln2_b": nrm(ks[18], (DEPTH, D_MODEL), 0.02),
        "ln3_g": 1.0 + nrm(ks[19], (DEPTH, D_MODEL), 0.05),
        "ln3_b": nrm(ks[20], (DEPTH, D_MODEL), 0.02),
        "ffn2_w_gate": nrm(ks[21], (DEPTH, D_MODEL, D_FF), D_MODEL ** -0.5),
        "ffn2_w_up": nrm(ks[22], (DEPTH, D_MODEL, D_FF), D_MODEL ** -0.5),
        "ffn2_w_down": nrm(ks[23], (DEPTH, D_FF, D_MODEL), BETA * D_FF ** -0.5),
    }


def reference(x_prompt, x_sample, cache_k_a, cache_v_a, cache_kidx_a, cache_k_b, cache_v_b, cache_logf_b,
              w_in, b_f, w_out, ln1_g, ln1_b, ffn1_w_gate, ffn1_w_up, ffn1_w_down,
              ln2_g, ln2_b, ln3_g, ln3_b, ffn2_w_gate, ffn2_w_up, ffn2_w_down):
    seq = x_prompt.shape[1]
    dec_seq = x_sample.shape[1]
    past = cache_k_a.shape[2]
    pos_p = jnp.arange(seq, dtype=jnp.int32)
    pos_s = past + jnp.arange(dec_seq, dtype=jnp.int32)
    xp = x_prompt
    xs = x_sample
    rows_p = []
    rows_s = []
    for l in range(DEPTH):
        hp = macaron_half(xp, ln1_g[l], ln1_b[l], ffn1_w_gate[l], ffn1_w_up[l], ffn1_w_down[l])
        qa, ka, va, qi, ki, wi, qb, kb, vb, lfb = project(hp, pos_p, w_in[l], b_f[l])
        mp = mixer_prompt(qa, ka, va, qi, ki, wi, qb, kb, vb, lfb)
        hp = layer_norm(ALPHA * hp + mp @ w_out[l], ln2_g[l], ln2_b[l])
        xp = macaron_half(hp, ln3_g[l], ln3_b[l], ffn2_w_gate[l], ffn2_w_up[l], ffn2_w_down[l])
        rows_p.append((ka, va, ki, kb, vb, lfb))
        hs = macaron_half(xs, ln1_g[l], ln1_b[l], ffn1_w_gate[l], ffn1_w_up[l], ffn1_w_down[l])
        qa, ka, va, qi, ki, wi, qb, kb, vb, lfb = project(hs, pos_s, w_in[l], b_f[l])
        ms = mixer_sample(qa, ka, va, qi, ki, wi, qb, kb, vb, lfb, pos_s,
                          cache_k_a[l], cache_v_a[l], cache_kidx_a[l], cache_k_b[l], cache_v_b[l], cache_logf_b[l])
        hs = layer_norm(ALPHA * hs + ms @ w_out[l], ln2_g[l], ln2_b[l])
        xs = macaron_half(hs, ln3_g[l], ln3_b[l], ffn2_w_gate[l], ffn2_w_up[l], ffn2_w_down[l])
        rows_s.append((ka, va, ki, kb, vb, lfb))
    new_k_a_p, new_v_a_p, new_kidx_p, new_k_b_p, new_v_b_p, new_logf_p = [jnp.stack(a, axis=0) for a in zip(*rows_p)]
    new_k_a_s, new_v_a_s, new_kidx_s, new_k_b_s, new_v_b_s, new_logf_s = [jnp.stack(a, axis=0) for a in zip(*rows_s)]
    y_prompt = xp
    y_sample = xs
    return (y_prompt, y_sample,
            new_k_a_p, new_v_a_p, new_kidx_p, new_k_b_p, new_v_b_p, new_logf_p,
            new_k_a_s, new_v_a_s, new_kidx_s, new_k_b_s, new_v_b_s, new_logf_s)
```

```python
import os
import numpy as np
from contextlib import ExitStack
import concourse.bass as bass
import concourse.mybir as mybir
from concourse.bass_utils import run_bass_kernel_spmd

F32 = mybir.dt.float32
BF16 = mybir.dt.bfloat16
ALU = mybir.AluOpType
AF = mybir.ActivationFunctionType
AX = mybir.AxisListType

COMPUTE = ("pe", "act", "dve", "pool")
QUEUES = ("sp",)


class Buf:
    __slots__ = ("name", "last_w", "readers", "sem", "sem_count", "extra", "excl")

    def __init__(self, name, excl=False):
        self.name = name
        self.excl = excl
        self.last_w = None
        self.readers = {}
        self.sem = None
        self.sem_count = 0
        self.extra = []


class Op:
    __slots__ = ("eng", "fn", "deps", "is_dma", "sem_buf", "dma_val", "idx", "signal", "signo")

    def __init__(self, eng, fn, is_dma=False, sem_buf=None):
        self.eng = eng
        self.fn = fn
        self.deps = []
        self.is_dma = is_dma
        self.sem_buf = sem_buf
        self.dma_val = 0
        self.idx = 0
        self.signal = False
        self.signo = 0


class Prog:
    def __init__(self, nc, stack):
        self.nc = nc
        self.stack = stack
        self.streams = {e: [] for e in COMPUTE + QUEUES}
        self.dma_bufs = []

    def _track(self, op, reads, writes):
        tok = ("d", op.sem_buf, op.dma_val) if op.is_dma else ("c", op)
        deps = op.deps
        rkey = ("d", id(op.sem_buf)) if op.is_dma else op.eng
        for b in reads:
            if b.last_w is not None:
                deps.append((b.last_w, "raw"))
            if b.excl:
                for k, t in b.readers.items():
                    if k != rkey:
                        deps.append((t, "war"))
        for b in writes:
            if b.last_w is not None:
                deps.append((b.last_w, "waw"))
            for t in b.readers.values():
                deps.append((t, "war"))
            if b.extra:
                for t in b.extra:
                    deps.append((t, "war"))
                b.extra = []
        key = ("d", id(op.sem_buf)) if op.is_dma else op.eng
        for b in reads:
            b.readers[key] = tok
        for b in writes:
            b.last_w = tok
            b.readers = {}

    def fence(self, srcs, dsts):
        toks = []
        for s in srcs:
            if s.last_w is not None:
                toks.append(s.last_w)
            toks.extend(s.readers.values())
        for d in dsts:
            d.extra.extend(toks)

    def op(self, eng, fn, reads=(), writes=()):
        o = Op(eng, fn)
        st = self.streams[eng]
        o.idx = len(st)
        self._track(o, reads, writes)
        st.append(o)
        return o

    def dma(self, queue, fn, reads=(), writes=(), sem_buf=None):
        if sem_buf.sem is None:
            sem_buf.sem = self.stack.enter_context(self.nc.semaphore("d_" + sem_buf.name))
            self.dma_bufs.append(sem_buf)
        sem_buf.sem_count += 16
        o = Op(queue, fn, is_dma=True, sem_buf=sem_buf)
        o.dma_val = sem_buf.sem_count
        st = self.streams[queue]
        o.idx = len(st)
        self._track(o, reads, writes)
        st.append(o)
        return o

    def emit(self, final_queue="sp"):
        nc = self.nc

        def skip(o, src, kind):
            return src.eng == o.eng and (o.eng == "pe" or kind != "raw") and not o.is_dma

        for e, st in self.streams.items():
            for o in st:
                for d, kind in o.deps:
                    if d[0] == "c" and not skip(o, d[1], kind):
                        d[1].signal = True
        esem = {}
        totals = {}
        for e in COMPUTE:
            n = 0
            for o in self.streams[e]:
                if not o.is_dma and o.signal:
                    n += 1
                    o.signo = n
            totals[e] = n
            if n:
                esem[e] = self.stack.enter_context(nc.semaphore("s_" + e))
        handles = {"pe": "tensor", "act": "scalar", "dve": "vector", "pool": "gpsimd", "sp": "sync"}
        block = self.stack.enter_context(nc.Block())

        def run_stream(e, eng):
            known = {}
            for o in self.streams[e]:
                waits = {}
                for d, kind in o.deps:
                    if d[0] == "c":
                        src = d[1]
                        if skip(o, src, kind):
                            continue
                        sem = esem[src.eng]
                        val = src.signo
                    else:
                        sem = d[1].sem
                        val = d[2]
                    key = id(sem)
                    if known.get(key, 0) >= val:
                        continue
                    if key not in waits or waits[key][1] < val:
                        waits[key] = (sem, val)
                for key, (sem, val) in waits.items():
                    eng.wait_ge(sem, val)
                    known[key] = val
                ins = o.fn(eng)
                if o.is_dma:
                    ins.then_inc(o.sem_buf.sem, 16)
                elif o.signal:
                    ins.then_inc(esem[e], 1)
            if e == final_queue:
                for b in self.dma_bufs:
                    if known.get(id(b.sem), 0) < b.sem_count:
                        eng.wait_ge(b.sem, b.sem_count)
                for ce, sem in esem.items():
                    if known.get(id(sem), 0) < totals[ce]:
                        eng.wait_ge(sem, totals[ce])

        for e in COMPUTE + QUEUES:
            if not self.streams[e] and e != final_queue:
                continue
            getattr(block, handles[e])(lambda eng, e=e: run_stream(e, eng))


D = 1024
DFF = 2816
NJ = 22
NIN = 2896
SEQ = 4096
NT_MAX = 33
SKEY = NT_MAX * 128
NIT = 12
NEG = -30000.0
ALPHA = 2.0 ** 0.25
C_FFN = 0.5 / ALPHA
C_OUT = 1.0 / ALPHA
LN_EPS = 1e-5 / (ALPHA * ALPHA)
KAPPA = (8.0 ** -0.5) * (64.0 ** -0.5)
KA, VA, KI, LF, Z5W = 0, 128, 256, 384, 392
PWI, PLF = 320, 328
N_CORES = 8


def _win_perm():
    qa = np.arange(0, 512).reshape(8, 64)
    qa = np.concatenate([qa[[c, 4 + c]].reshape(-1) for c in range(4)])
    ka = np.arange(512, 640)
    va = np.arange(640, 768)
    qi = np.arange(768, 1280)
    ki = np.arange(1280, 1344)
    wi = np.arange(1344, 1352)
    qb = np.arange(1352, 1864)
    kb = np.arange(1864, 2376)
    vb = np.arange(2376, 2888)
    fb = np.arange(2888, 2896)
    return np.concatenate([qa, qi, qb, kb, vb, ka, va, ki, wi, fb])


class Cfg:
    def __init__(self):
        self.nblk = int(os.environ.get("K_NBLK", "8"))
        self.nseq = int(os.environ.get("K_NSEQ", "2"))
        self.sample = int(os.environ.get("K_SAMPLE", "1"))
        self.stage = int(os.environ.get("K_STAGE", "9"))
        self.debug = int(os.environ.get("K_DEBUG", "0"))
        self.cut = int(os.environ.get("K_CUT", "99"))
        self.cores = int(os.environ.get("K_CORES", "8"))
        self.pg = [int(c) for c in os.environ.get("K_PG", "012345")]
        self.norope = int(os.environ.get("K_NOROPE", "0"))
        self.g5 = int(os.environ.get("K_G5", "255"))


def build_program(cfg):
    nc = bass.Bass("TRN2", target_bir_lowering=False)

    def din(name, shape, dt=F32):
        return nc.dram_tensor(name, list(shape), dt, kind="ExternalInput").ap()

    def dout(name, shape, dt=F32):
        return nc.dram_tensor(name, list(shape), dt, kind="ExternalOutput").ap()

    def dscr(name, shape, dt):
        return nc.dram_tensor(name, list(shape), dt, kind="Internal").ap()

    xp = din("xp", [2, SEQ, D])
    xs = din("xs", [4, 128, D])
    c_5 = din("c_5", [4, SEQ, Z5W])
    c_kb = din("c_kb", [4, SEQ, 512])
    c_vb = din("c_vb", [4, SEQ, 512])
    wgu_h = din("wgu_h", [2, NJ, 128, 2048])
    wd_h = din("wd_h", [2, DFF, D])
    win_h = din("win_h", [128, 8, NIN])
    wout_h = din("wout_h", [128, 8, D])
    lnp = din("lnp", [3, 2, 128, D])
    bf_h = din("bf_h", [128, 8])
    rope_h = din("rope_h", [SKEY, 16])

    y_p = dout("y_p", [2, SEQ, D])
    y_s = dout("y_s", [4, 64, D])
    o_ka_p = dout("ka_p", [2, SEQ, 128]); o_va_p = dout("va_p", [2, SEQ, 128])
    o_ki_p = dout("ki_p", [2, SEQ, 64]); o_kb_p = dout("kb_p", [2, SEQ, 512])
    o_vb_p = dout("vb_p", [2, SEQ, 512]); o_lf_p = dout("lf_p", [2, SEQ, 8])
    o_ka_s = dout("ka_s", [4, 64, 128]); o_va_s = dout("va_s", [4, 64, 128])
    o_ki_s = dout("ki_s", [4, 64, 64]); o_kb_s = dout("kb_s", [4, 64, 512])
    o_vb_s = dout("vb_s", [4, 64, 512]); o_lf_s = dout("lf_s", [4, 64, 8])
    if cfg.debug:
        dbg_mix = dout("dbg_mix", [8, 128, 512], BF16)
        dbg_h1 = dout("dbg_h1", [4, 128, D])

    wgu_s = dscr("scr_wgu_s", [2, NJ, 128, 2048], BF16)
    wd_s = dscr("scr_wd_s", [2, DFF, D], BF16)
    win_s = dscr("scr_win_s", [128, 8, NIN], BF16)
    wout_s = dscr("scr_wout_s", [128, 8, D], BF16)
    kTb_s = dscr("scr_kTb_s", [8, 65, SKEY], BF16)
    vb_s = dscr("scr_vb_s", [8, 128, NT_MAX, 128], BF16)
    zs_kb = dscr("scr_zs_kb", [4, 128, 512], F32)
    zs_vb = dscr("scr_zs_vb", [4, 128, 512], F32)
    zs_5 = dscr("scr_zs_5", [4, 128, Z5W], F32)
    wgu_v, wd_v, win_v, wout_v = wgu_s, wd_s, win_s, wout_s

    def flat2k(ap, pat):
        return ap.rearrange(pat + " -> (" + pat + ")").rearrange("(r e) -> r e", e=2048)

    stack = ExitStack()
    P = Prog(nc, stack)

    def sb(name, shape, dt):
        return stack.enter_context(nc.sbuf_tensor(name, list(shape), dt))

    res = sb("res", [128, 4, D], F32); b_res = [Buf("res%d" % i) for i in range(4)]
    aT = sb("aT", [128, 8, 512], BF16); b_aT = Buf("aT")
    UW = 128 * NT_MAX * 4 + 128 * NT_MAX * 2
    U = sb("U", [128, UW // 2], BF16)
    hT = U[:, 0:NJ * 512].rearrange("p (j t) -> p j t", j=NJ); b_hT = Buf("hT")
    score = U[:, 0:SKEY * 2].bitcast(F32); b_score = Buf("score")
    mask = U[:, SKEY * 2:SKEY * 3]; b_mask = Buf("mask")
    wsl = [sb("wsl%d" % i, [128, 4096], BF16) for i in range(3)]; b_wsl = [Buf("wsl%d" % i) for i in range(3)]
    zkb2 = [sb("zkb%d" % i, [128, 512], F32) for i in range(2)]; b_zkb2 = [Buf("zkb%d" % i) for i in range(2)]
    zvb2 = [sb("zvb%d" % i, [128, 512], F32) for i in range(2)]; b_zvb2 = [Buf("zvb%d" % i) for i in range(2)]
    z52 = [sb("z5%d" % i, [128, Z5W], F32) for i in range(2)]; b_z52 = [Buf("z5%d" % i) for i in range(2)]
    stq = [sb("stq%d" % i, [128, 512], BF16) for i in range(2)]; b_stq = [Buf("stq%d" % i) for i in range(2)]
    stb = sb("stb", [128, 4, 8, 65], BF16); b_stb = [Buf("stb%d" % i) for i in range(4)]
    stv2 = [sb("stv%d" % i, [128, 8, 128], BF16) for i in range(2)]; b_stv2 = [Buf("stv%d" % i) for i in range(2)]
    ktmp2 = [sb("ktmp%d" % i, [65, 8, 128], BF16) for i in range(2)]; b_ktmp2 = [Buf("ktmp%d" % i) for i in range(2)]
    kTa = sb("kTa", [128, SKEY], BF16); b_kTa = Buf("kTa")
    vpa = sb("vpa", [128, NT_MAX, 192], BF16); b_vpa = Buf("vpa")
    kiT = sb("kiT", [128, SKEY], BF16); b_kiT = Buf("kiT")
    qTa = sb("qTa", [128, 4, 512], BF16); b_qTa = Buf("qTa")
    qiT = sb("qiT", [128, 4, 512], BF16); b_qiT = Buf("qiT")
    qTb = sb("qTb", [65, 8, 512], BF16); b_qTb = Buf("qTb")
    rh = [sb("rh%d" % i, [128, 512], BF16) for i in range(4)]; b_rh = [Buf("rh%d" % i) for i in range(4)]
    Dm = sb("Dm", [128, 8, 128], BF16); b_Dm = Buf("Dm")
    ident4 = sb("ident4", [128, 512], BF16)
    junk8 = sb("junk8", [128, SKEY], mybir.dt.uint8); b_junk = Buf("junk8")
    qip = [sb("qip%d" % i, [128, 8, 128], BF16) for i in range(1)]; b_qip = [Buf("qip%d" % i) for i in range(1)]
    qap = [sb("qap%d" % i, [128, 8, 128], BF16) for i in range(1)]; b_qap = [Buf("qap%d" % i) for i in range(1)]
    trib = sb("trib", [128, 128], BF16)
    osb = [sb("osb%d" % i, [64, 512], F32) for i in range(1)]; b_osb = [Buf("osb%d" % i) for i in range(1)]
    rdl = [sb("rdl%d" % i, [64, 512], F32) for i in range(1)]; b_rdl = [Buf("rdl%d" % i) for i in range(1)]
    PT = [sb("PT%d" % i, [128, 512], BF16) for i in range(4)]; b_PT = [Buf("PT%d" % i) for i in range(4)]
    mixT = sb("mixT", [128, 8, 512], BF16); b_mixT = Buf("mixT")
    kch = [sb("kch%d" % i, [65, 1024], BF16) for i in range(3)]; b_kch = [Buf("kch%d" % i) for i in range(3)]
    vch = [sb("vch%d" % i, [128, 8, 128], BF16) for i in range(3)]; b_vch = [Buf("vch%d" % i) for i in range(3)]
    sg = [sb("sg%d" % i, [128, 512], F32) for i in range(1)]; b_sg = [Buf("sg%d" % i) for i in range(1)]
    identf = sb("identf", [128, 128], F32); b_c = Buf("consts")
    identb = sb("identb", [128, 128], BF16)
    onesf = sb("onesf", [128, 128], F32)
    Utri = sb("Utri", [128, 128], F32)
    Elast = sb("Elast", [128, 128], F32)
    tri = sb("tri", [128, 128], F32)
    admp = sb("admp", [128, 128], F32)
    adms = sb("adms", [128, 128], F32)
    pow2 = sb("pow2", [128, NIT], F32)
    rope = sb("rope", [128, NT_MAX, 16], F32)
    bfb = sb("bfb", [128, 8], F32)
    lng = sb("lng", [128, D], F32); b_lng = Buf("lng")
    lnb = sb("lnb", [128, D], F32); b_lnb = Buf("lnb")
    cum = sb("cum", [128, NT_MAX, 8], F32); b_cum = Buf("cum")
    biasb = sb("biasb", [128, NT_MAX, 8], F32); b_biasb = Buf("biasb")
    cbc = sb("cbc", [128, 8], F32); b_cbc = Buf("cbc")
    aab = sb("aab", [128, 4, 8], F32); b_aab = Buf("aab")
    sgn = sb("sgn", [128, 4, 8], F32); b_sgn = Buf("sgn")
    sm = sb("sm", [128, 64], F32); b_sm = Buf("sm"); b_thr = Buf("thr")
    rtmp = sb("rtmp", [128, 4, 64], F32); b_rtmp = Buf("rtmp")
    bst = sb("bst", [128, 4, 2, 6], F32)
    lns = sb("lns", [128, 4, 8], F32); b_ln = [Buf("ln%d" % i) for i in range(4)]
    xr = sb("xr", [128, 8, 16], F32); b_xr = Buf("xr")
    wf = [sb("wf%d" % i, [128, 16], F32) for i in range(2)]; b_wf = [Buf("wf%d" % i) for i in range(2)]
    b_sm2 = Buf("sm2")
    cons = sb("cons", [128, 4], F32)
    wtab = sb("wtab", [128, NIT], F32)

    psb = [stack.enter_context(nc.psum_tensor("ps%d" % i, [128, 512], F32)) for i in range(8)]
    b_ps = [Buf("ps%d" % i, excl=True) for i in range(8)]
    b_scr_w = Buf("wscr")
    b_ks = [Buf("ks%d" % i) for i in range(5)]
    b_vs = [Buf("vs%d" % i) for i in range(5)]
    b_zs = [Buf("zs%d" % i) for i in range(4)]
    b_out = Buf("outs")

    def mm(out, lhsT, rhs, start, stop, r, w):
        P.op("pe", lambda e: e.matmul(out, lhsT=lhsT, rhs=rhs, start=start, stop=stop), r, w)

    def tr(out, in_, ident, r, w):
        P.op("pe", lambda e: e.transpose(out=out, in_=in_, identity=ident), r, w)

    def act(out, in_, func, r, w, bias=None, scale=None):
        kw = {}
        if bias is not None:
            kw["bias"] = bias
        if scale is not None:
            kw["scale"] = scale
        P.op("act", lambda e: e.activation(out=out, in_=in_, func=func, **kw), r, w)

    def cp(eng, out, in_, r, w):
        if eng == "act":
            P.op("act", lambda e: e.activation(out=out, in_=in_, func=AF.Copy), r, w)
        else:
            P.op(eng, lambda e: e.tensor_copy(out=out, in_=in_), r, w)

    def tt(eng, out, in0, in1, op, r, w):
        P.op(eng, lambda e: e.tensor_tensor(out=out, in0=in0, in1=in1, op=op), r, w)

    def ts(eng, out, in0, s1, s2, op0, op1, r, w, accum=None):
        if op1 is None:
            P.op(eng, lambda e: e.tensor_scalar(out=out, in0=in0, scalar1=s1, scalar2=None, op0=op0), r, w)
        elif accum is None:
            P.op(eng, lambda e: e.tensor_scalar(out=out, in0=in0, scalar1=s1, scalar2=s2, op0=op0, op1=op1), r, w)
        else:
            P.op(eng, lambda e: e.tensor_scalar(out=out, in0=in0, scalar1=s1, scalar2=s2, op0=op0, op1=op1,
                                                accum_out=accum), r, w)

    def stt(out, in0, scalar, in1, op0, op1, r, w):
        P.op("dve", lambda e: e.scalar_tensor_tensor(out=out, in0=in0, scalar=scalar, in1=in1, op0=op0, op1=op1), r, w)

    def memset(eng, ap, v, w):
        P.op(eng, lambda e: e.memset(ap, v), (), w)

    def dma(q, out, in_, r, w, sem):
        P.dma(q, lambda e: e.dma_start(out=out, in_=in_), r, w, sem_buf=sem)

    def psbf(i):
        return psb[i][:].bitcast(BF16)

    def aff(ap, pattern, cmp, fill, base, cm):
        P.op("pool", lambda e: e.affine_select(out=ap, in_=ap, pattern=pattern, compare_op=cmp, fill=fill,
                                               base=base, channel_multiplier=cm), (b_c,), (b_c,))

    memset("pool", identf[:], 0.0, (b_c,))
    aff(identf[:], [[-1, 128]], ALU.not_equal, 1.0, 0, 1)
    cp("pool", identb[:], identf[:], (b_c,), (b_c,))
    memset("pool", onesf[:], 1.0, (b_c,))
    memset("pool", Utri[:], 1.0, (b_c,))
    aff(Utri[:], [[1, 128]], ALU.is_ge, 0.0, 0, -1)
    memset("pool", Elast[:], 0.0, (b_c,))
    aff(Elast[:], [[0, 128]], ALU.not_equal, 1.0, -127, 1)
    memset("pool", tri[:], 0.0, (b_c,))
    aff(tri[:], [[1, 128]], ALU.is_ge, NEG, 0, -1)
    cp("pool", trib[:], tri[:], (b_c,), (b_c,))
    for i2 in range(1):
        memset("pool", qip[i2][:], 0.0, (b_qip[i2],))
        memset("pool", qap[i2][:], 0.0, (b_qap[i2],))
    for j4 in range(4):
        cp("pool", ident4[:, j4 * 128:(j4 + 1) * 128], identb[:], (b_c,), (b_c,))
    memset("pool", admp[:], 0.0, (b_c,))
    memset("pool", admp[0:64, 64:128], NEG, (b_c,))
    memset("pool", adms[:], 0.0, (b_c,))
    memset("pool", adms[:, 64:128], NEG, (b_c,))
    for k in range(NIT):
        memset("pool", pow2[:, k:k + 1], 2.0 ** -(k + 1), (b_c,))
    memset("pool", cons[:, 0:1], 1.0, (b_c,))
    memset("pool", cons[:, 1:2], LN_EPS, (b_c,))
    for i2 in range(2):
        memset("pool", ktmp2[i2][:], 1.0, (b_ktmp2[i2],))
        memset("pool", stv2[i2][:], 1.0, (b_stv2[i2],))
    memset("pool", stb[:], 1.0, tuple(b_stb))
    memset("pool", vpa[:], 1.0, (b_vpa,))
    memset("pool", cum[:], 0.0, (b_cum,))
    dma("sp", rope[:], rope_h.rearrange("(t p) c -> p t c", p=128), (), (b_c,), b_c)
    dma("sp", bfb[:], bf_h[:, :], (), (b_c,), b_c)

    def cast_rows(dst, src, nrows):
        step = 2048
        for r0 in range(0, nrows, step):
            r1 = min(nrows, r0 + step)
            dma("pool", dst[r0:r1, :], src[r0:r1, :], (), (b_scr_w,), b_scr_w)

    cast_rows(flat2k(wgu_s, "f j p e"), flat2k(wgu_h, "f j p e"), 2 * NJ * 128)
    cast_rows(flat2k(win_s, "p c x"), flat2k(win_h, "p c x"), 128 * 8 * NIN // 2048)
    cast_rows(flat2k(wd_s, "f r d"), flat2k(wd_h, "f r d"), 2 * DFF * D // 2048)
    cast_rows(flat2k(wout_s, "p c d"), flat2k(wout_h, "p c d"), 128 * 8 * D // 2048)

    wctr = [0]

    def wslot():
        i = wctr[0] % 3
        wctr[0] += 1
        return wsl[i], b_wsl[i]

    rot = {"a": 0, "t": 0, "q": 0, "o": 0, "rh": 0, "pt": 0, "kc": 0, "vc": 0, "sg": 0, "stq": 0, "nb": 0, "qp": 0, "zb": 0}

    def nxt(key, n):
        v = rot[key] % n
        rot[key] += 1
        return v

    def transpose_res(ntt):
        for t in range(ntt):
            for c0 in (0, 4):
                bk = 4 + nxt("t", 4)
                for cc in range(4):
                    c = c0 + cc
                    tr(psb[bk][:, cc * 128:(cc + 1) * 128], res[:, t, c * 128:(c + 1) * 128], identf[:],
                       (b_res[t], b_c), (b_ps[bk],))
                cp("act" if c0 == 0 else "dve", aT[:, c0:c0 + 4, t * 128:(t + 1) * 128],
                   psb[bk][:, :].rearrange("p (c x) -> p c x", c=4), (b_ps[bk],), (b_aT,))

    pre = {0: [], 1: []}

    def load_gu(f, jp):
        w, bw = wslot()
        dma("sp", w[:].rearrange("p (j e) -> p j e", j=2), wgu_v[f, 2 * jp:2 * jp + 2].rearrange("j p e -> p j e"),
            (b_scr_w,), (bw,), bw)
        return w, bw

    def prefetch_ffn(f, n=3):
        for jp in range(n):
            pre[f].append(load_gu(f, jp))

    def ffn(f, ntt):
        N = ntt * 128
        for jp in range(NJ // 2):
            if pre[f]:
                w, bw = pre[f].pop(0)
            else:
                w, bw = load_gu(f, jp)
            wv = w[:].rearrange("p (j s c f) -> p j s c f", j=2, s=2, c=8)
            for jj in range(2):
                j = 2 * jp + jj
                bg = nxt("a", 2) * 2
                for s in range(2):
                    for c in range(8):
                        mm(psb[bg + s][:, 0:N], wv[:, jj, s, c, :], aT[:, c, 0:N], c == 0, c == 7,
                           (bw, b_aT), (b_ps[bg + s],))
                si = 0
                act(sg[si][:, 0:N], psb[bg][:, 0:N], AF.Silu, (b_ps[bg],), (b_sg[si],))
                tt("dve", hT[:, j, 0:N], sg[si][:, 0:N], psb[bg + 1][:, 0:N], ALU.mult,
                   (b_sg[si], b_ps[bg + 1]), (b_hT,))
        for jg in range((NJ + 3) // 4):
            nj = min(4, NJ - 4 * jg)
            w, bw = wslot()
            dma("sp", w[:, 0:nj * D].rearrange("p (j d) -> p j d", j=nj),
                wd_v[f, 512 * jg:512 * jg + 128 * nj, :].rearrange("(j p) d -> p j d", p=128), (b_scr_w,), (bw,), bw)
            wv = w[:].rearrange("p (j d) -> p j d", j=4)
            for jj in range(nj):
                j = 4 * jg + jj
                for t in range(ntt):
                    for hf in range(2):
                        mm(psb[2 * t + hf][:, :], hT[:, j, t * 128:(t + 1) * 128], wv[:, jj, hf * 512:(hf + 1) * 512],
                           j == 0, j == NJ - 1, (bw, b_hT), (b_ps[2 * t + hf],))
        for t in range(ntt):
            for hf in range(2):
                sl = res[:, t, hf * 512:(hf + 1) * 512]
                stt(sl, psb[2 * t + hf][:, :], C_FFN, sl, ALU.mult, ALU.add, (b_ps[2 * t + hf], b_res[t]), (b_res[t],))

    def layer_norm(l, ntt):
        dma("sp", lng[:], lnp[l, 0], (), (b_lng,), b_lng)
        dma("sp", lnb[:], lnp[l, 1], (), (b_lnb,), b_lnb)
        L0 = 0
        for t in range(ntt):
            for hf in range(2):
                P.op("dve", lambda e, t=t, hf=hf: e.bn_stats(out=bst[:, t, hf, :], in_=res[:, t, hf * 512:(hf + 1) * 512]),
                     (b_res[t],), (b_ln[t],))
        for t in range(ntt):
            P.op("dve", lambda e, t=t: e.bn_aggr(out=lns[:, t, 0:2], in_=bst[:, t, :, :].rearrange("p a b -> p (a b)")),
                 (b_ln[t],), (b_ln[t],))
        for t in range(ntt):
            act(lns[:, t, 2:3], lns[:, t, 1:2], AF.Ln, (b_ln[t], b_c), (b_ln[t],), bias=cons[:, 1:2])
        for t in range(ntt):
            act(lns[:, t, 3:4], lns[:, t, 2:3], AF.Exp, (b_ln[t],), (b_ln[t],), scale=-0.5)
        for t in range(ntt):
            ts("dve", lns[:, t, 4:5], lns[:, t, 0:1], -1.0, lns[:, t, 3:4], ALU.mult, ALU.mult, (b_ln[t],), (b_ln[t],))
        for t in range(ntt):
            act(res[:, t, :], res[:, t, :], AF.Identity, (b_res[t], b_ln[t]), (b_res[t],), bias=lns[:, t, 4:5], scale=lns[:, t, 3:4])
        for t in range(ntt):
            tt("pool", res[:, t, :], res[:, t, :], lng[:], ALU.mult, (b_res[t], b_lng), (b_res[t],))
            tt("dve", res[:, t, :], res[:, t, :], lnb[:], ALU.add, (b_res[t], b_lnb), (b_res[t],))

    def rope_fix(src, dst, H, T, r, w):
        if cfg.norope == 1:
            return
        sv = src.rearrange("p (h d) -> p h d", h=H)
        dv = dst.rearrange("p (h d) -> p h d", h=H)
        cosb = rope[:, T, 0:8].unsqueeze(1).broadcast_to([128, H, 8])
        sinb = rope[:, T, 8:16].unsqueeze(1).broadcast_to([128, H, 8])
        tv = rtmp[:].rearrange("p a (h d) -> p a h d", h=8)
        t1, t2, t3, t4 = (tv[:, i, 0:H, :] for i in range(4))
        xv = xr[:, 0:H, :]
        act(xv, sv[:, :, 0:16], AF.Copy, r, (b_xr,))
        if cfg.norope == 2:
            return
        rr = (b_xr, b_c)
        tt("pool", t1, xv[:, :, 0:8], cosb, ALU.mult, rr, (b_rtmp,))
        tt("pool", t2, xv[:, :, 8:16], sinb, ALU.mult, rr, (b_rtmp,))
        tt("pool", t3, xv[:, :, 8:16], cosb, ALU.mult, rr, (b_rtmp,))
        tt("pool", t4, xv[:, :, 0:8], sinb, ALU.mult, rr, (b_rtmp,))
        if cfg.norope == 3:
            return
        fe = "dve"
        tt(fe, dv[:, :, 0:8], t1, t2, ALU.subtract, (b_rtmp,), w)
        tt(fe, dv[:, :, 8:16], t3, t4, ALU.add, (b_rtmp,), w)

    zsel = {"kb": 0, "vb": 0, "z5": 0}

    def znext(kind):
        zsel[kind] = (zsel[kind] + 1) % 2
        return zsel[kind]

    def ingest_kb(T, q="pool", e0="act"):
        i2 = zsel["kb"]
        zkb, b_zkb, ktmp, b_ktmp = zkb2[i2], b_zkb2[i2], ktmp2[i2], b_ktmp2[i2]
        bk = 4 + nxt("t", 4)
        for c in range(4):
            tr(psb[bk][:, c * 128:(c + 1) * 128], zkb[:, c * 128:(c + 1) * 128], identf[:], (b_zkb, b_c), (b_ps[bk],))
        kv = ktmp[:].rearrange("r (c two) k -> r c two k", two=2)
        cp(e0, kv[0:64, :, 0, :], psb[bk][0:64, :].rearrange("p (c k) -> p c k", c=4), (b_ps[bk],), (b_ktmp,))
        cp("dve", kv[0:64, :, 1, :], psb[bk][64:128, :].rearrange("p (c k) -> p c k", c=4), (b_ps[bk],), (b_ktmp,))
        dma(q, kTb_s[:, :, T * 128:(T + 1) * 128].rearrange("h r c -> r h c"), ktmp[:], (b_ktmp,), (b_ks[T // 8],), b_ktmp)

    def ingest_vb(T, q="pool", ce="act"):
        i2 = zsel["vb"]
        zvb, b_zvb, stv, b_stv = zvb2[i2], b_zvb2[i2], stv2[i2], b_stv2[i2]
        cp(ce, stv[:, :, 0:64], zvb[:].rearrange("p (h d) -> p h d", h=8), (b_zvb,), (b_stv,))
        dma(q, vb_s[:, :, T, :].rearrange("h p c -> p h c"), stv[:], (b_stv,), (b_vs[T // 8],), b_stv)

    def ingest_5(T, i2=None):
        if i2 is None:
            i2 = zsel["z5"]
        z5, b_z5 = z52[i2], b_z52[i2]
        if cfg.g5 & 32:
            bk = 4 + nxt("t", 4)
            tr(psb[bk][:, 0:128], z5[:, KA:KA + 128], identf[:], (b_z5, b_c), (b_ps[bk],))
            tr(psb[bk][:, 128:256], z5[:, KI:KI + 128], identf[:], (b_z5, b_c), (b_ps[bk],))
            cp("act", kTa[:, T * 128:(T + 1) * 128], psb[bk][:, 0:128], (b_ps[bk],), (b_kTa,))
            cp("dve", kiT[:, T * 128:(T + 1) * 128], psb[bk][:, 128:256], (b_ps[bk],), (b_kiT,))
        if cfg.g5 & 64:
            cp("dve", vpa[:, T, 0:64], z5[:, VA:VA + 64], (b_z5,), (b_vpa,))
            cp("dve", vpa[:, T, 128:192], z5[:, VA + 64:VA + 128], (b_z5,), (b_vpa,))
        if cfg.g5 & 128:
            bk2 = 4 + nxt("t", 4)
            mm(psb[bk2][:, 0:8], Utri[:], z5[:, LF:LF + 8], True, T == 0, (b_z5, b_c), (b_ps[bk2],))
            if T > 0:
                mm(psb[bk2][:, 0:8], Elast[:], cum[:, T - 1, :], False, True, (b_cum, b_c), (b_ps[bk2],))
            cp("dve", cum[:, T, :], psb[bk2][:, 0:8], (b_ps[bk2],), (b_cum,))

    def g5_pass2(t, T, i2, outs):
        z5, b_z5 = z52[i2], b_z52[i2]
        w_ = wf[i2]
        act(aab[:, t, :], w_[:, 0:8], AF.Abs, (b_wf[i2],), (b_aab,), scale=KAPPA)
        act(sgn[:, t, :], w_[:, 0:8], AF.Sign, (b_wf[i2],), (b_sgn,))
        tt("dve", sm[:, 8:16], w_[:, 8:16], bfb[:], ALU.add, (b_wf[i2], b_c), (b_sm2,))
        act(sm[:, 16:24], sm[:, 8:16], AF.Exp, (b_sm2,), (b_sm2,), scale=-1.0)
        act(sm[:, 24:32], sm[:, 16:24], AF.Ln, (b_sm2, b_c), (b_sm2,), bias=cons[:, 0:1])
        ts("dve", z5[:, LF:LF + 8], sm[:, 24:32], -1.0, None, ALU.mult, None, (b_sm2,), (b_z5,))
        if outs is not None:
            n = outs["rows"]
            dma("pool", outs["ka"][t], z5[0:n, KA:KA + 128], (b_z5,), (b_out,), b_z5)
            dma("pool", outs["va"][t], z5[0:n, VA:VA + 128], (b_z5,), (b_out,), b_z5)
            dma("pool", outs["ki"][t], z5[0:n, KI:KI + 64], (b_z5,), (b_out,), b_z5)
            dma("pool", outs["lf"][t], z5[0:n, LF:LF + 8], (b_z5,), (b_out,), b_z5)
        if outs is None or not outs.get("defer"):
            ingest_5(T, i2)
        else:
            dma("pool", zs_5[t], z5[:], (b_z5,), (b_zs[t],), b_z5)

    def project(ntt, Ts, outs):
        goffs = [0, 512, 1024, 1536, 2048, 2560]
        gw = [512, 512, 512, 512, 512, 336]
        for g in cfg.pg:
            w, bw = wslot()
            wv = w[:, 0:8 * gw[g]].rearrange("p (c x) -> p c x", c=8)
            dma("sp", wv, win_v[:, :, goffs[g]:goffs[g] + gw[g]], (b_scr_w,), (bw,), bw)
            for t in range(ntt):
                T = Ts[t]
                bk = nxt("a", 4)
                pw = psb[bk][:, 0:gw[g]]
                for c in range(8):
                    mm(pw, aT[:, c, t * 128:(t + 1) * 128], wv[:, c, :], c == 0, c == 7, (bw, b_aT), (b_ps[bk],))
                bp = b_ps[bk]
                if g in (0, 1):
                    si = nxt("stq", 2)
                    cp("act", stq[si][:], pw, (bp,), (b_stq[si],))
                    rope_fix(pw, stq[si][:], 8, T, (bp,), (b_stq[si],))
                    bk2 = 4 + nxt("t", 4)
                    pv = psbf(bk2)
                    for c in range(4):
                        tr(pv[:, c * 128:(c + 1) * 128], stq[si][:, c * 128:(c + 1) * 128], identb[:],
                           (b_stq[si], b_c), (b_ps[bk2],))
                    dst, bd = (qTa, b_qTa) if g == 0 else (qiT, b_qiT)
                    cp("dve", dst[:, :, t * 128:(t + 1) * 128], pv[:, 0:512].rearrange("p (c x) -> p c x", c=4),
                       (b_ps[bk2],), (bd,))
                elif g == 2:
                    cp("act", stb[:, t, :, 0:64], pw.rearrange("p (h d) -> p h d", h=8), (bp,), (b_stb[t],))
                elif g == 3:
                    i2 = znext("kb"); zkb, b_zkb = zkb2[i2], b_zkb2[i2]
                    cp("act", zkb[:], pw, (bp,), (b_zkb,))
                    if outs is not None:
                        dma("pool", outs["kb"][t], zkb[0:outs["rows"], :], (b_zkb,), (b_out,), b_zkb)
                    if outs is None or not outs.get("defer"):
                        ingest_kb(T, e0="dve")
                    else:
                        dma("pool", zs_kb[t], zkb[:], (b_zkb,), (b_zs[t],), b_zkb)
                elif g == 4:
                    i2 = znext("vb"); zvb, b_zvb = zvb2[i2], b_zvb2[i2]
                    cp("act", zvb[:], pw, (bp,), (b_zvb,))
                    if outs is not None:
                        dma("pool", outs["vb"][t], zvb[0:outs["rows"], :], (b_zvb,), (b_out,), b_zvb)
                    if outs is None or not outs.get("defer"):
                        ingest_vb(T, ce="pool")
                    else:
                        dma("pool", zs_vb[t], zvb[:], (b_zvb,), (b_zs[t],), b_zvb)
                else:
                    i2 = znext("z5"); z5, b_z5 = z52[i2], b_z52[i2]
                    cp("act", z5[:, 0:320], pw[:, 0:320], (bp,), (b_z5,))
                    cp("act", wf[i2][:], pw[:, 320:336], (bp,), (b_wf[i2],))
                    rope_fix(pw[:, 0:128], z5[:, KA:KA + 128], 2, T, (bp,), (b_z5,))
                    rope_fix(pw[:, 256:320], z5[:, KI:KI + 64], 1, T, (bp,), (b_z5,))
                    cp("dve", z5[:, KI + 64:KI + 128], z5[:, KI:KI + 64], (b_z5,), (b_z5,))
                    if t > 0:
                        g5_pass2(t - 1, Ts[t - 1], 1 - i2, outs)
                    if t == ntt - 1:
                        g5_pass2(t, T, i2, outs)

    def fox_prep(T0, nT, tts, Ts):
        bk = 4 + nxt("t", 4)
        mm(psb[bk][:, 0:8], onesf[:], cum[:, T0, :], True, True, (b_cum, b_c), (b_ps[bk],))
        act(cbc[:], psb[bk][:, 0:8], AF.Copy, (b_ps[bk],), (b_cbc,), scale=1.0 / 128.0)
        tt("dve", biasb[:, 0:nT, :], cbc[:].unsqueeze(1).broadcast_to([128, nT, 8]), cum[:, 0:nT, :], ALU.subtract,
           (b_cbc, b_cum), (b_biasb,))
        for t, T in zip(tts, Ts):
            tt("dve", sm[:, 32:40], cum[:, T, :], cbc[:], ALU.subtract, (b_cum, b_cbc), (b_sm,))
            act(stb[:, t, :, 64:65], sm[:, 32:40].unsqueeze(2), AF.Copy, (b_sm,), (b_stb[t],), scale=8.0)
            bk = 4 + nxt("t", 4)
            pv = psbf(bk)
            for h in range(8):
                tr(pv[0:65, h * 128:(h + 1) * 128], stb[:, t, h, :], identb[:], (b_stb[t], b_c), (b_ps[bk],))
            cp("dve", qTb[:, :, t * 128:(t + 1) * 128], pv[0:65, 0:1024].rearrange("r (h c) -> r h c", h=8),
               (b_ps[bk],), (b_qTb,))

    LA = 3
    QB = [0, 1, 2, 6, 7]

    def pipeline(n, A, C):
        for i in range(n + LA):
            if i < n:
                A(i)
            if i >= LA:
                C(i - LA)

    def prep_idx(t):
        for h in range(8):
            ts("pool", Dm[:, h, :], identb[:], sgn[:, t, h:h + 1], None, ALU.mult, None, (b_c, b_sgn), (b_Dm,))
        qv = qip[0][:].rearrange("p (c two) q -> p c two q", two=2)
        cp("pool", qv[0:64, :, 0, :], qiT[0:64, :, t * 128:(t + 1) * 128], (b_qiT,), (b_qip[0],))
        cp("pool", qv[64:128, :, 1, :], qiT[64:128, :, t * 128:(t + 1) * 128], (b_qiT,), (b_qip[0],))

    def prep_dsa(t):
        cp("pool", qap[0][0:64, 0:4, :], qTa[0:64, :, t * 128:(t + 1) * 128], (b_qTa,), (b_qap[0],))
        cp("pool", qap[0][64:128, 4:8, :], qTa[64:128, :, t * 128:(t + 1) * 128], (b_qTa,), (b_qap[0],))

    def indexer_scores(t, T, sample):
        S = 128 * (T + 1)
        qi_ = 0
        items = [(c0, h) for c0 in range(0, S, 512) for h in range(8)]
        st = {}

        def A(i):
            c0, h = items[i]
            wd = min(512, S - c0)
            bk = QB[nxt("q", 5)]
            mm(psb[bk][:, 0:wd], qip[qi_][:, h, :], kiT[:, c0:c0 + wd], True, True, (b_qip[qi_], b_kiT), (b_ps[bk],))
            ri = nxt("rh", 4)
            act(rh[ri][:, 0:wd], psb[bk][:, 0:wd], AF.Relu, (b_ps[bk], b_aab), (b_rh[ri],), scale=aab[:, t, h:h + 1])
            st[i] = ri

        def C(i):
            c0, h = items[i]
            wd = min(512, S - c0)
            ri = st.pop(i)
            mm(psb[3][:, 0:wd], Dm[:, h, :], rh[ri][:, 0:wd], h == 0, h == 7, (b_Dm, b_rh[ri]), (b_ps[3],))
            if h == 7:
                cp("dve", score[:, c0:c0 + wd], psb[3][:, 0:wd], (b_ps[3],), (b_score,))

        pipeline(len(items), A, C)

    def bisect(t, T, sample):
        S = 128 * (T + 1)
        sc = score[:, 0:S]
        P.op("dve", lambda e: e.tensor_reduce(out=sm[:, 40:41], in_=sc, axis=AX.X, op=ALU.max), (b_score,), (b_sm,))
        P.op("dve", lambda e: e.tensor_reduce(out=sm[:, 41:42], in_=sc, axis=AX.X, op=ALU.min), (b_score,), (b_sm,))
        ts("dve", sm[:, 42:43], sm[:, 40:41], sm[:, 41:42], 1.001, ALU.subtract, ALU.mult, (b_sm,), (b_sm,))
        ts("dve", sm[:, 42:43], sm[:, 42:43], 2e-3, None, ALU.add, None, (b_sm,), (b_sm,))
        ts("dve", sm[:, 43:44], sm[:, 41:42], -1e-3, None, ALU.add, None, (b_sm,), (b_sm,))
        ts("dve", wtab[:], pow2[:], sm[:, 42:43], None, ALU.mult, None, (b_sm, b_c), (b_sm,))
        tt("dve", score[:, S - 128:S], score[:, S - 128:S], (adms if sample else admp)[:], ALU.add, (b_score, b_c), (b_score,))
        for k in range(NIT):
            tt("dve", sm[:, 44:45], sm[:, 43:44], wtab[:, k:k + 1], ALU.add, (b_sm,), (b_sm,))
            ts("dve", junk8[:, 0:S], sc, sm[:, 44:45], None, ALU.is_ge, ALU.add, (b_score, b_sm), (b_junk, b_sm),
               accum=sm[:, 45:46])
            ts("dve", sm[:, 46:47], sm[:, 45:46], 255.5, wtab[:, k:k + 1], ALU.is_ge, ALU.mult, (b_sm,), (b_sm,))
            tt("dve", sm[:, 43:44], sm[:, 43:44], sm[:, 46:47], ALU.add, (b_sm,), (b_sm,))
        cp("dve", sm[:, 48 + (t % 4):49 + (t % 4)], sm[:, 43:44], (b_sm,), (b_thr,))

    def bisect_final(t, T):
        S = 128 * (T + 1)
        ts("dve", mask[:, 0:S], score[:, 0:S], sm[:, 48 + (t % 4):49 + (t % 4)], NEG, ALU.is_lt, ALU.mult,
           (b_score, b_thr), (b_mask,))

    def normalize(bk, orow, drow, c0n, c1n, dsts):
        ni = 0
        act(osb[ni][:, c0n:c1n], psb[bk][orow:orow + 64, c0n:c1n], AF.Copy, (b_ps[bk],), (b_osb[ni],))
        act(rdl[ni][:, c0n:c1n], psb[bk][drow:drow + 64, c0n:c1n], AF.Ln, (b_ps[bk],), (b_rdl[ni],))
        act(rdl[ni][:, c0n:c1n], rdl[ni][:, c0n:c1n], AF.Exp, (b_rdl[ni],), (b_rdl[ni],), scale=-1.0)
        for (c0, c1, dst) in dsts:
            tt("pool", dst, osb[ni][:, c0:c1], rdl[ni][:, c0:c1], ALU.mult, (b_osb[ni], b_rdl[ni]), (b_mixT,))

    def dsa(t, T):
        qa_ = 0
        for g in range(2):
            bo = 4 + (g % 2)
            st = {}

            def A(k):
                bk = QB[nxt("q", 5)]
                mm(psb[bk][:, :], kTa[:, k * 128:(k + 1) * 128], qap[qa_][:, 4 * g:4 * g + 4, :],
                   True, False, (b_kTa, b_qap[qa_]), (b_ps[bk],))
                mm(psb[bk][:, :], mask[:, k * 128:(k + 1) * 128], ident4[:], False, True, (b_mask, b_c), (b_ps[bk],))
                pi = nxt("pt", 4)
                act(PT[pi][:], psb[bk][:, :], AF.Exp, (b_ps[bk],), (b_PT[pi],), scale=0.125)
                st[k] = pi

            def C(k):
                pi = st.pop(k)
                lhs = vpa[:, k, 0:128] if g == 0 else vpa[:, k, 64:192]
                mm(psb[bo][:, :], lhs, PT[pi][:], k == 0, k == T, (b_vpa, b_PT[pi]), (b_ps[bo],))

            pipeline(T + 1, A, C)
            orow, drow = (0, 64) if g == 0 else (64, 0)
            dsts = []
            for j in range(4):
                dsts.append((j * 128, (j + 1) * 128, mixT[64 * (j % 2):64 * (j % 2) + 64, 2 * g + j // 2, t * 128:(t + 1) * 128]))
            normalize(bo, orow, drow, 0, 512, dsts)

    def fox(t0, nq, T0, heads):
        nT = T0 + nq
        Q0 = t0 * 128
        NQ = nq * 128
        for h in heads:
            bo = 4 + (h % 2)
            chunks = {}
            st = {}

            def load(cidx):
                nt = min(8, nT - 8 * cidx)
                ki = nxt("kc", 3)
                dma("sp", kch[ki][:, 0:nt * 128], kTb_s[h, :, cidx * 1024:cidx * 1024 + nt * 128], (b_ks[cidx],),
                    (b_kch[ki],), b_kch[ki])
                vi = nxt("vc", 3)
                dma("sp", vch[vi][:, 0:nt, :], vb_s[h, :, 8 * cidx:8 * cidx + nt, :], (b_vs[cidx],), (b_vch[vi],), b_vch[vi])
                chunks[cidx] = (ki, vi)

            def A(s):
                cidx, ss = s // 8, s % 8
                if cidx not in chunks:
                    load(cidx)
                ki, vi = chunks[cidx]
                kk = max(0, s - T0)
                c0, c1 = kk * 128, NQ
                bk = QB[nxt("q", 5)]
                diag = s >= T0
                mm(psb[bk][:, c0:c1], kch[ki][0:65, ss * 128:(ss + 1) * 128], qTb[0:65, h, Q0 + c0:Q0 + c1],
                   True, not diag, (b_kch[ki], b_qTb), (b_ps[bk],))
                if diag:
                    mm(psb[bk][:, c0:c0 + 128], identb[:], trib[:], False, True, (b_c,), (b_ps[bk],))
                pi = nxt("pt", 4)
                act(PT[pi][:, c0:c1], psb[bk][:, c0:c1], AF.Exp, (b_ps[bk], b_biasb), (b_PT[pi],),
                    bias=biasb[:, s, h:h + 1], scale=0.125)
                st[s] = (pi, vi, ss, c0, c1)

            def C(s):
                pi, vi, ss, c0, c1 = st.pop(s)
                mm(psb[bo][:, c0:c1], vch[vi][:, ss, :], PT[pi][:, c0:c1], s == 0, s == nT - 1,
                   (b_vch[vi], b_PT[pi]), (b_ps[bo],))

            pipeline(nT, A, C)
            dst = mixT[64 * (h % 2):64 * (h % 2) + 64, 4 + h // 2, Q0:Q0 + NQ]
            normalize(bo, 0, 64, 0, NQ, [(0, NQ, dst)])

    def out_proj(ntt):
        for hf in range(2):
            w, bw = wslot()
            wv = w[:].rearrange("p (c x) -> p c x", c=8)
            dma("sp", wv, wout_v[:, :, hf * 512:(hf + 1) * 512], (b_scr_w,), (bw,), bw)
            for t in range(ntt):
                bk = nxt("a", 4)
                for c in range(8):
                    mm(psb[bk][:, :], mixT[:, c, t * 128:(t + 1) * 128], wv[:, c, :], c == 0, c == 7, (bw, b_mixT), (b_ps[bk],))
                sl = res[:, t, hf * 512:(hf + 1) * 512]
                stt(sl, psb[bk][:, :], C_OUT, sl, ALU.mult, ALU.add, (b_ps[bk], b_res[t]), (b_res[t],))

    def block_front(ntt, Ts, outs):
        if cfg.cut >= 1:
            transpose_res(ntt)
        P.fence([b_score, b_mask], [b_hT])
        if cfg.cut >= 2:
            ffn(0, ntt)
        if cfg.cut >= 3:
            layer_norm(0, ntt)
        if cfg.cut >= 4:
            transpose_res(ntt)
        P.fence([b_hT], [b_score, b_mask])
        if cfg.cut >= 5:
            project(ntt, Ts, outs)

    def block_back(ntt, more):
        out_proj(ntt)
        layer_norm(1, ntt)
        transpose_res(ntt)
        P.fence([b_score, b_mask], [b_hT])
        ffn(1, ntt)
        if more:
            prefetch_ffn(0)
        layer_norm(2, ntt)
        P.fence([b_hT], [b_score, b_mask])

    for b in range(cfg.nseq if cfg.stage >= 1 else 0):
        for i in range(cfg.nblk):
            r0 = 512 * i
            for t4 in range(4):
                dma("sp", res[:, t4, :], xp[b, r0 + 128 * t4:r0 + 128 * (t4 + 1), :], (), (b_res[t4],), b_res[t4])
            Ts = [4 * i + t for t in range(4)]
            rows = lambda o: [o[b, r0 + 128 * t:r0 + 128 * (t + 1), :] for t in range(4)]
            outs = {"rows": 128, "ka": rows(o_ka_p), "va": rows(o_va_p), "ki": rows(o_ki_p), "kb": rows(o_kb_p),
                    "vb": rows(o_vb_p), "lf": rows(o_lf_p)}
            block_front(4, Ts, outs)
            if cfg.debug and b == 0 and i == 0:
                dma("pool", dbg_h1.rearrange("t p d -> p t d"), res[:], tuple(b_res), (b_out,), b_res[1])
            if cfg.stage >= 2:
                fox_prep(4 * i, 4 * i + 4, range(4), Ts)
                prep_idx(0)
                indexer_scores(0, 4 * i, False)
                prep_idx(1)
                bisect(0, 4 * i, False)
                prep_dsa(0)
                for t in range(4):
                    fox(0, 4, 4 * i, [2 * t, 2 * t + 1])
                    bisect_final(t, 4 * i + t)
                    if t + 1 < 4:
                        indexer_scores(t + 1, 4 * i + t + 1, False)
                        if t + 2 < 4:
                            prep_idx(t + 2)
                        bisect(t + 1, 4 * i + t + 1, False)
                    dsa(t, 4 * i + t)
                    if t + 1 < 4:
                        prep_dsa(t + 1)
                if cfg.debug and b == 0 and i == cfg.nblk - 1:
                    dma("pool", dbg_mix.rearrange("c p x -> p c x"), mixT[:], (b_mixT,), (b_out,), b_mixT)
            if cfg.stage >= 9:
                last = (b == cfg.nseq - 1 and i == cfg.nblk - 1)
                block_back(4, (not last) or bool(cfg.sample))
                for t4 in range(4):
                    dma("pool", y_p[b, r0 + 128 * t4:r0 + 128 * (t4 + 1), :], res[:, t4, :], (b_res[t4],), (b_out,), b_res[t4])

    if cfg.sample:
        dma("sp", res[:], xs.rearrange("t p d -> p t d"), (), tuple(b_res), b_res[0])
        srow = lambda o: [o[t] for t in range(4)]
        outs = {"rows": 64, "defer": True, "ka": srow(o_ka_s), "va": srow(o_va_s), "ki": srow(o_ki_s),
                "kb": srow(o_kb_s), "vb": srow(o_vb_s), "lf": srow(o_lf_s)}
        block_front(4, [32] * 4, outs)
        if cfg.stage >= 2:
            for r in range(4):
                for T in range(33):
                    k0 = 128 * T
                    i2 = znext("kb"); zkb, b_zkb = zkb2[i2], b_zkb2[i2]
                    if T < 32:
                        dma("sp", zkb[:], c_kb[r, k0:k0 + 128, :], (), (b_zkb,), b_zkb)
                    else:
                        dma("sp", zkb[:], zs_kb[r], (b_zs[r],), (b_zkb,), b_zkb)
                    ingest_kb(T)
                    i2 = znext("vb"); zvb, b_zvb = zvb2[i2], b_zvb2[i2]
                    if T < 32:
                        dma("sp", zvb[:], c_vb[r, k0:k0 + 128, :], (), (b_zvb,), b_zvb)
                    else:
                        dma("sp", zvb[:], zs_vb[r], (b_zs[r],), (b_zvb,), b_zvb)
                    ingest_vb(T)
                    i2 = znext("z5"); z5, b_z5 = z52[i2], b_z52[i2]
                    if T < 32:
                        dma("sp", z5[:], c_5[r, k0:k0 + 128, :], (), (b_z5,), b_z5)
                    else:
                        dma("sp", z5[:], zs_5[r], (b_zs[r],), (b_z5,), b_z5)
                    ingest_5(T)
                fox_prep(32, 33, [r], [32])
                prep_idx(r)
                indexer_scores(r, 32, True)
                prep_dsa(r)
                bisect(r, 32, True)
                fox(r, 1, 32, list(range(8)))
                bisect_final(r, 32)
                dsa(r, 32)
        if cfg.stage >= 9:
            block_back(4, False)
            dma("pool", y_s.rearrange("t p d -> p t d"), res[0:64, :, :], tuple(b_res), (b_out,), b_res[1])

    P.emit(final_queue="sp")
    stack.close()
    return nc, {e: len(s) for e, s in P.streams.items()}


_PROG = {}


def _get_prog():
    cfg = Cfg()
    key = (cfg.nblk, cfg.nseq, cfg.sample, cfg.stage, cfg.debug, cfg.cut, tuple(cfg.pg), cfg.norope, cfg.g5)
    if key not in _PROG:
        _PROG[key] = (build_program(cfg), cfg)
    return _PROG[key]


def _rope_table():
    half = 8
    inv = (np.float32(500000.0) ** (-np.arange(half, dtype=np.float32) * np.float32(2.0) / np.float32(16.0))).astype(np.float32)
    pos = np.arange(SKEY, dtype=np.float32)
    ang = (pos[:, None] * inv[None, :]).astype(np.float32)
    return np.concatenate([np.cos(ang), np.sin(ang)], axis=1).astype(np.float32)


def kernel(x_prompt, x_sample, cache_k_a, cache_v_a, cache_kidx_a, cache_k_b, cache_v_b, cache_logf_b,
           w_in, b_f, w_out, ln1_g, ln1_b, ffn1_w_gate, ffn1_w_up, ffn1_w_down,
           ln2_g, ln2_b, ln3_g, ln3_b, ffn2_w_gate, ffn2_w_up, ffn2_w_down):
    (nc, _), cfg = _get_prog()
    f32 = np.float32
    A = lambda a: np.ascontiguousarray(np.asarray(a, dtype=f32))
    def gu(wg, wu):
        g = A(wg)[0].reshape(8, 128, NJ, 128)
        u = A(wu)[0].reshape(8, 128, NJ, 128)
        s = np.stack([g, u], axis=0)
        return s.transpose(3, 2, 0, 1, 4)
    wgu = np.stack([gu(ffn1_w_gate, ffn1_w_up), gu(ffn2_w_gate, ffn2_w_up)], axis=0).reshape(2, NJ, 128, 2048)
    wd = np.stack([A(ffn1_w_down)[0], A(ffn2_w_down)[0]], axis=0)
    win = A(w_in)[0][:, _win_perm()].reshape(8, 128, NIN).transpose(1, 0, 2)
    wout = A(w_out)[0].reshape(8, 128, D).transpose(1, 0, 2)
    lnp = np.stack([np.stack([A(g)[0], A(bb)[0]]) for g, bb in ((ln1_g, ln1_b), (ln2_g, ln2_b), (ln3_g, ln3_b))])
    lnp = np.ascontiguousarray(np.broadcast_to(lnp[:, :, None, :], (3, 2, 128, D)))
    bfh = np.ascontiguousarray(np.broadcast_to(A(b_f)[0][None, :], (128, 8)))
    ropeh = _rope_table()
    shared = {"wgu_h": np.ascontiguousarray(wgu), "wd_h": np.ascontiguousarray(wd), "win_h": np.ascontiguousarray(win),
              "wout_h": np.ascontiguousarray(wout), "lnp": lnp, "bf_h": bfh, "rope_h": ropeh}
    xp = A(x_prompt)
    xsm = A(x_sample)
    in_maps = []
    for c in range(N_CORES):
        xs_pad = np.zeros((4, 128, D), f32)
        xs_pad[:, 0:64, :] = xsm[4 * c:4 * c + 4]
        m = dict(shared)
        m.update({
            "xp": np.ascontiguousarray(xp[2 * c:2 * c + 2]),
            "xs": xs_pad,
            "c_5": np.ascontiguousarray(np.concatenate([
                A(cache_k_a)[0, 4 * c:4 * c + 4].reshape(4, SEQ, 128), A(cache_v_a)[0, 4 * c:4 * c + 4].reshape(4, SEQ, 128),
                A(cache_kidx_a)[0, 4 * c:4 * c + 4].reshape(4, SEQ, 64), A(cache_kidx_a)[0, 4 * c:4 * c + 4].reshape(4, SEQ, 64),
                A(cache_logf_b)[0, 4 * c:4 * c + 4].reshape(4, SEQ, 8)], axis=2)),
            "c_kb": A(cache_k_b)[0, 4 * c:4 * c + 4].reshape(4, SEQ, 512),
            "c_vb": A(cache_v_b)[0, 4 * c:4 * c + 4].reshape(4, SEQ, 512),
        })
        in_maps.append(m)
    if cfg.cores < N_CORES:
        res = run_bass_kernel_spmd(nc, in_maps[:cfg.cores], core_ids=list(range(cfg.cores)))
        R = list(res.results) + [res.results[0]] * (N_CORES - cfg.cores)
    else:
        res = run_bass_kernel_spmd(nc, in_maps, core_ids=list(range(N_CORES)))
        R = res.results
    cat = lambda k: np.concatenate([np.asarray(r[k]) for r in R], axis=0)
    y_prompt = cat("y_p")
    y_sample = cat("y_s")
    outs = [y_prompt, y_sample,
            cat("ka_p").reshape(1, 16, SEQ, 2, 64), cat("va_p").reshape(1, 16, SEQ, 2, 64),
            cat("ki_p").reshape(1, 16, SEQ, 64), cat("kb_p").reshape(1, 16, SEQ, 8, 64),
            cat("vb_p").reshape(1, 16, SEQ, 8, 64), cat("lf_p").reshape(1, 16, SEQ, 8),
            cat("ka_s").reshape(1, 32, 64, 2, 64), cat("va_s").reshape(1, 32, 64, 2, 64),
            cat("ki_s").reshape(1, 32, 64, 64), cat("kb_s").reshape(1, 32, 64, 8, 64),
            cat("vb_s").reshape(1, 32, 64, 8, 64), cat("lf_s").reshape(1, 32, 64, 8)]
    if cfg.debug:
        kernel.debug = {k: [np.asarray(r[k]) for r in R] for k in ("dbg_mix", "dbg_h1")}
    return tuple(np.ascontiguousarray(o.astype(f32)) for o in outs)
```

```python
import os
import numpy as np
from contextlib import ExitStack
import concourse.bass as bass
import concourse.mybir as mybir
from concourse.bass_utils import run_bass_kernel_spmd

F32 = mybir.dt.float32
BF16 = mybir.dt.bfloat16
ALU = mybir.AluOpType
AF = mybir.ActivationFunctionType
AX = mybir.AxisListType

COMPUTE = ("pe", "act", "dve", "pool")
QUEUES = ("sp",)


class Buf:
    __slots__ = ("name", "last_w", "readers", "sem", "sem_count", "extra", "excl")

    def __init__(self, name, excl=False):
        self.name = name
        self.excl = excl
        self.last_w = None
        self.readers = {}
        self.sem = None
        self.sem_count = 0
        self.extra = []


class Op:
    __slots__ = ("eng", "fn", "deps", "is_dma", "sem_buf", "dma_val", "idx", "signal", "signo")

    def __init__(self, eng, fn, is_dma=False, sem_buf=None):
        self.eng = eng
        self.fn = fn
        self.deps = []
        self.is_dma = is_dma
        self.sem_buf = sem_buf
        self.dma_val = 0
        self.idx = 0
        self.signal = False
        self.signo = 0


class Prog:
    def __init__(self, nc, stack):
        self.nc = nc
        self.stack = stack
        self.streams = {e: [] for e in COMPUTE + QUEUES}
        self.dma_bufs = []

    def _track(self, op, reads, writes):
        tok = ("d", op.sem_buf, op.dma_val) if op.is_dma else ("c", op)
        deps = op.deps
        rkey = ("d", id(op.sem_buf)) if op.is_dma else op.eng
        for b in reads:
            if b.last_w is not None:
                deps.append((b.last_w, "raw"))
            if b.excl:
                for k, t in b.readers.items():
                    if k != rkey:
                        deps.append((t, "war"))
        for b in writes:
            if b.last_w is not None:
                deps.append((b.last_w, "waw"))
            for t in b.readers.values():
                deps.append((t, "war"))
            if b.extra:
                for t in b.extra:
                    deps.append((t, "war"))
                b.extra = []
        key = ("d", id(op.sem_buf)) if op.is_dma else op.eng
        for b in reads:
            b.readers[key] = tok
        for b in writes:
            b.last_w = tok
            b.readers = {}

    def fence(self, srcs, dsts):
        toks = []
        for s in srcs:
            if s.last_w is not None:
                toks.append(s.last_w)
            toks.extend(s.readers.values())
        for d in dsts:
            d.extra.extend(toks)

    def op(self, eng, fn, reads=(), writes=()):
        o = Op(eng, fn)
        st = self.streams[eng]
        o.idx = len(st)
        self._track(o, reads, writes)
        st.append(o)
        return o

    def dma(self, queue, fn, reads=(), writes=(), sem_buf=None):
        if sem_buf.sem is None:
            sem_buf.sem = self.stack.enter_context(self.nc.semaphore("d_" + sem_buf.name))
            self.dma_bufs.append(sem_buf)
        sem_buf.sem_count += 16
        o = Op(queue, fn, is_dma=True, sem_buf=sem_buf)
        o.dma_val = sem_buf.sem_count
        st = self.streams[queue]
        o.idx = len(st)
        self._track(o, reads, writes)
        st.append(o)
        return o

    def emit(self, final_queue="sp"):
        nc = self.nc

        def skip(o, src, kind):
            return src.eng == o.eng and (o.eng == "pe" or kind != "raw") and not o.is_dma

        for e, st in self.streams.items():
            for o in st:
                for d, kind in o.deps:
                    if d[0] == "c" and not skip(o, d[1], kind):
                        d[1].signal = True
        esem = {}
        totals = {}
        for e in COMPUTE:
            n = 0
            for o in self.streams[e]:
                if not o.is_dma and o.signal:
                    n += 1
                    o.signo = n
            totals[e] = n
            if n:
                esem[e] = self.stack.enter_context(nc.semaphore("s_" + e))
        handles = {"pe": "tensor", "act": "scalar", "dve": "vector", "pool": "gpsimd", "sp": "sync"}
        block = self.stack.enter_context(nc.Block())

        def run_stream(e, eng):
            known = {}
            for o in self.streams[e]:
                waits = {}
                for d, kind in o.deps:
                    if d[0] == "c":
                        src = d[1]
                        if skip(o, src, kind):
                            continue
                        sem = esem[src.eng]
                        val = src.signo
                    else:
                        sem = d[1].sem
                        val = d[2]
                    key = id(sem)
                    if known.get(key, 0) >= val:
                        continue
                    if key not in waits or waits[key][1] < val:
                        waits[key] = (sem, val)
                for key, (sem, val) in waits.items():
                    eng.wait_ge(sem, val)
                    known[key] = val
                ins = o.fn(eng)
                if o.is_dma:
                    ins.then_inc(o.sem_buf.sem, 16)
                elif o.signal:
                    ins.then_inc(esem[e], 1)
            if e == final_queue:
                for b in self.dma_bufs:
                    if known.get(id(b.sem), 0) < b.sem_count:
                        eng.wait_ge(b.sem, b.sem_count)
                for ce, sem in esem.items():
                    if known.get(id(sem), 0) < totals[ce]:
                        eng.wait_ge(sem, totals[ce])

        for e in COMPUTE + QUEUES:
            if not self.streams[e] and e != final_queue:
                continue
            getattr(block, handles[e])(lambda eng, e=e: run_stream(e, eng))


D = 1024
DFF = 2816
NJ = 22
NIN = 2896
SEQ = 4096
NT_MAX = 33
SKEY = NT_MAX * 128
NIT = 12
NEG = -30000.0
ALPHA = 2.0 ** 0.25
C_FFN = 0.5 / ALPHA
C_OUT = 1.0 / ALPHA
LN_EPS = 1e-5 / (ALPHA * ALPHA)
KAPPA = (8.0 ** -0.5) * (64.0 ** -0.5)
KA, VA, KI, LF, Z5W = 0, 128, 256, 384, 392
PWI, PLF = 320, 328
N_CORES = 8


def _win_perm():
    qa = np.arange(0, 512).reshape(8, 64)
    qa = np.concatenate([qa[[c, 4 + c]].reshape(-1) for c in range(4)])
    ka = np.arange(512, 640)
    va = np.arange(640, 768)
    qi = np.arange(768, 1280)
    ki = np.arange(1280, 1344)
    wi = np.arange(1344, 1352)
    qb = np.arange(1352, 1864)
    kb = np.arange(1864, 2376)
    vb = np.arange(2376, 2888)
    fb = np.arange(2888, 2896)
    return np.concatenate([qa, qi, qb, kb, vb, ka, va, ki, wi, fb])


class Cfg:
    def __init__(self):
        self.nblk = int(os.environ.get("K_NBLK", "8"))
        self.nseq = int(os.environ.get("K_NSEQ", "2"))
        self.sample = int(os.environ.get("K_SAMPLE", "1"))
        self.stage = int(os.environ.get("K_STAGE", "9"))
        self.debug = int(os.environ.get("K_DEBUG", "0"))
        self.cut = int(os.environ.get("K_CUT", "99"))
        self.cores = int(os.environ.get("K_CORES", "8"))
        self.pg = [int(c) for c in os.environ.get("K_PG", "012345")]
        self.norope = int(os.environ.get("K_NOROPE", "0"))
        self.g5 = int(os.environ.get("K_G5", "255"))


def build_program(cfg):
    nc = bass.Bass("TRN2", target_bir_lowering=False)

    def din(name, shape, dt=F32):
        return nc.dram_tensor(name, list(shape), dt, kind="ExternalInput").ap()

    def dout(name, shape, dt=F32):
        return nc.dram_tensor(name, list(shape), dt, kind="ExternalOutput").ap()

    def dscr(name, shape, dt):
        return nc.dram_tensor(name, list(shape), dt, kind="Internal").ap()

    xp = din("xp", [2, SEQ, D])
    xs = din("xs", [4, 128, D])
    c_5 = din("c_5", [4, SEQ, Z5W])
    c_kb = din("c_kb", [4, SEQ, 512])
    c_vb = din("c_vb", [4, SEQ, 512])
    wgu_h = din("wgu_h", [2, NJ, 128, 2048])
    wd_h = din("wd_h", [2, DFF, D])
    win_h = din("win_h", [128, 8, NIN])
    wout_h = din("wout_h", [128, 8, D])
    lnp = din("lnp", [3, 2, 128, D])
    bf_h = din("bf_h", [128, 8])
    rope_h = din("rope_h", [SKEY, 16])

    y_p = dout("y_p", [2, SEQ, D])
    y_s = dout("y_s", [4, 64, D])
    o_ka_p = dout("ka_p", [2, SEQ, 128]); o_va_p = dout("va_p", [2, SEQ, 128])
    o_ki_p = dout("ki_p", [2, SEQ, 64]); o_kb_p = dout("kb_p", [2, SEQ, 512])
    o_vb_p = dout("vb_p", [2, SEQ, 512]); o_lf_p = dout("lf_p", [2, SEQ, 8])
    o_ka_s = dout("ka_s", [4, 64, 128]); o_va_s = dout("va_s", [4, 64, 128])
    o_ki_s = dout("ki_s", [4, 64, 64]); o_kb_s = dout("kb_s", [4, 64, 512])
    o_vb_s = dout("vb_s", [4, 64, 512]); o_lf_s = dout("lf_s", [4, 64, 8])
    if cfg.debug:
        dbg_mix = dout("dbg_mix", [8, 128, 512], BF16)
        dbg_h1 = dout("dbg_h1", [4, 128, D])

    wgu_s = dscr("scr_wgu_s", [2, NJ, 128, 2048], BF16)
    wd_s = dscr("scr_wd_s", [2, DFF, D], BF16)
    win_s = dscr("scr_win_s", [128, 8, NIN], BF16)
    wout_s = dscr("scr_wout_s", [128, 8, D], BF16)
    kTb_s = dscr("scr_kTb_s", [8, 65, SKEY], BF16)
    vb_s = dscr("scr_vb_s", [8, 128, NT_MAX, 128], BF16)
    zs_kb = dscr("scr_zs_kb", [4, 128, 512], F32)
    zs_vb = dscr("scr_zs_vb", [4, 128, 512], F32)
    zs_5 = dscr("scr_zs_5", [4, 128, Z5W], F32)
    wgu_v, wd_v, win_v, wout_v = wgu_s, wd_s, win_s, wout_s

    def flat2k(ap, pat):
        return ap.rearrange(pat + " -> (" + pat + ")").rearrange("(r e) -> r e", e=2048)

    stack = ExitStack()
    P = Prog(nc, stack)

    def sb(name, shape, dt):
        return stack.enter_context(nc.sbuf_tensor(name, list(shape), dt))

    res = sb("res", [128, 4, D], F32); b_res = [Buf("res%d" % i) for i in range(4)]
    aT = sb("aT", [128, 8, 512], BF16); b_aT = Buf("aT")
    UW = 128 * NT_MAX * 4 + 128 * NT_MAX * 2
    U = sb("U", [128, UW // 2], BF16)
    hT = U[:, 0:NJ * 512].rearrange("p (j t) -> p j t", j=NJ); b_hT = Buf("hT")
    score = U[:, 0:SKEY * 2].bitcast(F32); b_score = Buf("score")
    mask = U[:, SKEY * 2:SKEY * 3]; b_mask = Buf("mask")
    wsl = [sb("wsl%d" % i, [128, 4096], BF16) for i in range(3)]; b_wsl = [Buf("wsl%d" % i) for i in range(3)]
    zkb2 = [sb("zkb%d" % i, [128, 512], F32) for i in range(2)]; b_zkb2 = [Buf("zkb%d" % i) for i in range(2)]
    zvb2 = [sb("zvb%d" % i, [128, 512], F32) for i in range(2)]; b_zvb2 = [Buf("zvb%d" % i) for i in range(2)]
    z52 = [sb("z5%d" % i, [128, Z5W], F32) for i in range(2)]; b_z52 = [Buf("z5%d" % i) for i in range(2)]
    stq = [sb("stq%d" % i, [128, 512], BF16) for i in range(2)]; b_stq = [Buf("stq%d" % i) for i in range(2)]
    stb = sb("stb", [128, 4, 8, 65], BF16); b_stb = [Buf("stb%d" % i) for i in range(4)]
    stv2 = [sb("stv%d" % i, [128, 8, 128], BF16) for i in range(2)]; b_stv2 = [Buf("stv%d" % i) for i in range(2)]
    ktmp2 = [sb("ktmp%d" % i, [65, 8, 128], BF16) for i in range(2)]; b_ktmp2 = [Buf("ktmp%d" % i) for i in range(2)]
    kTa = sb("kTa", [128, SKEY], BF16); b_kTa = Buf("kTa")
    vpa = sb("vpa", [128, NT_MAX, 192], BF16); b_vpa = Buf("vpa")
    kiT = sb("kiT", [128, SKEY], BF16); b_kiT = Buf("kiT")
    qTa = sb("qTa", [128, 4, 512], BF16); b_qTa = Buf("qTa")
    qiT = sb("qiT", [128, 4, 512], BF16); b_qiT = Buf("qiT")
    qTb = sb("qTb", [65, 8, 512], BF16); b_qTb = Buf("qTb")
    rh = [sb("rh%d" % i, [128, 512], BF16) for i in range(4)]; b_rh = [Buf("rh%d" % i) for i in range(4)]
    Dm = sb("Dm", [128, 8, 128], BF16); b_Dm = Buf("Dm")
    ident4 = sb("ident4", [128, 512], BF16)
    junk8 = sb("junk8", [128, SKEY], mybir.dt.uint8); b_junk = Buf("junk8")
    qip = [sb("qip%d" % i, [128, 8, 128], BF16) for i in range(1)]; b_qip = [Buf("qip%d" % i) for i in range(1)]
    qap = [sb("qap%d" % i, [128, 8, 128], BF16) for i in range(1)]; b_qap = [Buf("qap%d" % i) for i in range(1)]
    trib = sb("trib", [128, 128], BF16)
    osb = [sb("osb%d" % i, [64, 512], F32) for i in range(1)]; b_osb = [Buf("osb%d" % i) for i in range(1)]
    rdl = [sb("rdl%d" % i, [64, 512], F32) for i in range(1)]; b_rdl = [Buf("rdl%d" % i) for i in range(1)]
    PT = [sb("PT%d" % i, [128, 512], BF16) for i in range(4)]; b_PT = [Buf("PT%d" % i) for i in range(4)]
    mixT = sb("mixT", [128, 8, 512], BF16); b_mixT = Buf("mixT")
    kch = [sb("kch%d" % i, [65, 1024], BF16) for i in range(3)]; b_kch = [Buf("kch%d" % i) for i in range(3)]
    vch = [sb("vch%d" % i, [128, 8, 128], BF16) for i in range(3)]; b_vch = [Buf("vch%d" % i) for i in range(3)]
    sg = [sb("sg%d" % i, [128, 512], F32) for i in range(1)]; b_sg = [Buf("sg%d" % i) for i in range(1)]
    identf = sb("identf", [128, 128], F32); b_c = Buf("consts")
    identb = sb("identb", [128, 128], BF16)
    onesf = sb("onesf", [128, 128], F32)
    Utri = sb("Utri", [128, 128], F32)
    Elast = sb("Elast", [128, 128], F32)
    tri = sb("tri", [128, 128], F32)
    admp = sb("admp", [128, 128], F32)
    adms = sb("adms", [128, 128], F32)
    pow2 = sb("pow2", [128, NIT], F32)
    rope = sb("rope", [128, NT_MAX, 16], F32)
    bfb = sb("bfb", [128, 8], F32)
    lng = sb("lng", [128, D], F32); b_lng = Buf("lng")
    lnb = sb("lnb", [128, D], F32); b_lnb = Buf("lnb")
    cum = sb("cum", [128, NT_MAX, 8], F32); b_cum = Buf("cum")
    biasb = sb("biasb", [128, NT_MAX, 8], F32); b_biasb = Buf("biasb")
    cbc = sb("cbc", [128, 8], F32); b_cbc = Buf("cbc")
    aab = sb("aab", [128, 4, 8], F32); b_aab = Buf("aab")
    sgn = sb("sgn", [128, 4, 8], F32); b_sgn = Buf("sgn")
    sm = sb("sm", [128, 64], F32); b_sm = Buf("sm"); b_thr = Buf("thr")
    rtmp = sb("rtmp", [128, 4, 64], F32); b_rtmp = Buf("rtmp")
    bst = sb("bst", [128, 4, 2, 6], F32)
    lns = sb("lns", [128, 4, 8], F32); b_ln = [Buf("ln%d" % i) for i in range(4)]
    xr = sb("xr", [128, 8, 16], F32); b_xr = Buf("xr")
    wf = [sb("wf%d" % i, [128, 16], F32) for i in range(2)]; b_wf = [Buf("wf%d" % i) for i in range(2)]
    b_sm2 = Buf("sm2")
    cons = sb("cons", [128, 4], F32)
    wtab = sb("wtab", [128, NIT], F32)

    psb = [stack.enter_context(nc.psum_tensor("ps%d" % i, [128, 512], F32)) for i in range(8)]
    b_ps = [Buf("ps%d" % i, excl=True) for i in range(8)]
    b_scr_w = Buf("wscr")
    b_ks = [Buf("ks%d" % i) for i in range(5)]
    b_vs = [Buf("vs%d" % i) for i in range(5)]
    b_zs = [Buf("zs%d" % i) for i in range(4)]
    b_out = Buf("outs")

    def mm(out, lhsT, rhs, start, stop, r, w):
        P.op("pe", lambda e: e.matmul(out, lhsT=lhsT, rhs=rhs, start=start, stop=stop), r, w)

    def tr(out, in_, ident, r, w):
        P.op("pe", lambda e: e.transpose(out=out, in_=in_, identity=ident), r, w)

    def act(out, in_, func, r, w, bias=None, scale=None):
        kw = {}
        if bias is not None:
            kw["bias"] = bias
        if scale is not None:
            kw["scale"] = scale
        P.op("act", lambda e: e.activation(out=out, in_=in_, func=func, **kw), r, w)

    def cp(eng, out, in_, r, w):
        if eng == "act":
            P.op("act", lambda e: e.activation(out=out, in_=in_, func=AF.Copy), r, w)
        else:
            P.op(eng, lambda e: e.tensor_copy(out=out, in_=in_), r, w)

    def tt(eng, out, in0, in1, op, r, w):
        P.op(eng, lambda e: e.tensor_tensor(out=out, in0=in0, in1=in1, op=op), r, w)

    def ts(eng, out, in0, s1, s2, op0, op1, r, w, accum=None):
        if op1 is None:
            P.op(eng, lambda e: e.tensor_scalar(out=out, in0=in0, scalar1=s1, scalar2=None, op0=op0), r, w)
        elif accum is None:
            P.op(eng, lambda e: e.tensor_scalar(out=out, in0=in0, scalar1=s1, scalar2=s2, op0=op0, op1=op1), r, w)
        else:
            P.op(eng, lambda e: e.tensor_scalar(out=out, in0=in0, scalar1=s1, scalar2=s2, op0=op0, op1=op1,
                                                accum_out=accum), r, w)

    def stt(out, in0, scalar, in1, op0, op1, r, w):
        P.op("dve", lambda e: e.scalar_tensor_tensor(out=out, in0=in0, scalar=scalar, in1=in1, op0=op0, op1=op1), r, w)

    def memset(eng, ap, v, w):
        P.op(eng, lambda e: e.memset(ap, v), (), w)

    def dma(q, out, in_, r, w, sem):
        P.dma(q, lambda e: e.dma_start(out=out, in_=in_), r, w, sem_buf=sem)

    def psbf(i):
        return psb[i][:].bitcast(BF16)

    def aff(ap, pattern, cmp, fill, base, cm):
        P.op("pool", lambda e: e.affine_select(out=ap, in_=ap, pattern=pattern, compare_op=cmp, fill=fill,
                                               base=base, channel_multiplier=cm), (b_c,), (b_c,))

    memset("pool", identf[:], 0.0, (b_c,))
    aff(identf[:], [[-1, 128]], ALU.not_equal, 1.0, 0, 1)
    cp("pool", identb[:], identf[:], (b_c,), (b_c,))
    memset("pool", onesf[:], 1.0, (b_c,))
    memset("pool", Utri[:], 1.0, (b_c,))
    aff(Utri[:], [[1, 128]], ALU.is_ge, 0.0, 0, -1)
    memset("pool", Elast[:], 0.0, (b_c,))
    aff(Elast[:], [[0, 128]], ALU.not_equal, 1.0, -127, 1)
    memset("pool", tri[:], 0.0, (b_c,))
    aff(tri[:], [[1, 128]], ALU.is_ge, NEG, 0, -1)
    cp("pool", trib[:], tri[:], (b_c,), (b_c,))
    for i2 in range(1):
        memset("pool", qip[i2][:], 0.0, (b_qip[i2],))
        memset("pool", qap[i2][:], 0.0, (b_qap[i2],))
    for j4 in range(4):
        cp("pool", ident4[:, j4 * 128:(j4 + 1) * 128], identb[:], (b_c,), (b_c,))
    memset("pool", admp[:], 0.0, (b_c,))
    memset("pool", admp[0:64, 64:128], NEG, (b_c,))
    memset("pool", adms[:], 0.0, (b_c,))
    memset("pool", adms[:, 64:128], NEG, (b_c,))
    for k in range(NIT):
        memset("pool", pow2[:, k:k + 1], 2.0 ** -(k + 1), (b_c,))
    memset("pool", cons[:, 0:1], 1.0, (b_c,))
    memset("pool", cons[:, 1:2], LN_EPS, (b_c,))
    for i2 in range(2):
        memset("pool", ktmp2[i2][:], 1.0, (b_ktmp2[i2],))
        memset("pool", stv2[i2][:], 1.0, (b_stv2[i2],))
    memset("pool", stb[:], 1.0, tuple(b_stb))
    memset("pool", vpa[:], 1.0, (b_vpa,))
    memset("pool", cum[:], 0.0, (b_cum,))
    dma("sp", rope[:], rope_h.rearrange("(t p) c -> p t c", p=128), (), (b_c,), b_c)
    dma("sp", bfb[:], bf_h[:, :], (), (b_c,), b_c)

    def cast_rows(dst, src, nrows):
        step = 2048
        for r0 in range(0, nrows, step):
            r1 = min(nrows, r0 + step)
            dma("pool", dst[r0:r1, :], src[r0:r1, :], (), (b_scr_w,), b_scr_w)

    cast_rows(flat2k(wgu_s, "f j p e"), flat2k(wgu_h, "f j p e"), 2 * NJ * 128)
    cast_rows(flat2k(win_s, "p c x"), flat2k(win_h, "p c x"), 128 * 8 * NIN // 2048)
    cast_rows(flat2k(wd_s, "f r d"), flat2k(wd_h, "f r d"), 2 * DFF * D // 2048)
    cast_rows(flat2k(wout_s, "p c d"), flat2k(wout_h, "p c d"), 128 * 8 * D // 2048)

    wctr = [0]

    def wslot():
        i = wctr[0] % 3
        wctr[0] += 1
        return wsl[i], b_wsl[i]

    rot = {"a": 0, "t": 0, "q": 0, "o": 0, "rh": 0, "pt": 0, "kc": 0, "vc": 0, "sg": 0, "stq": 0, "nb": 0, "qp": 0, "zb": 0}

    def nxt(key, n):
        v = rot[key] % n
        rot[key] += 1
        return v

    def transpose_res(ntt):
        for t in range(ntt):
            for c0 in (0, 4):
                bk = 4 + nxt("t", 4)
                for cc in range(4):
                    c = c0 + cc
                    tr(psb[bk][:, cc * 128:(cc + 1) * 128], res[:, t, c * 128:(c + 1) * 128], identf[:],
                       (b_res[t], b_c), (b_ps[bk],))
                cp("act" if c0 == 0 else "dve", aT[:, c0:c0 + 4, t * 128:(t + 1) * 128],
                   psb[bk][:, :].rearrange("p (c x) -> p c x", c=4), (b_ps[bk],), (b_aT,))

    pre = {0: [], 1: []}

    def load_gu(f, jp):
        w, bw = wslot()
        dma("sp", w[:].rearrange("p (j e) -> p j e", j=2), wgu_v[f, 2 * jp:2 * jp + 2].rearrange("j p e -> p j e"),
            (b_scr_w,), (bw,), bw)
        return w, bw

    def prefetch_ffn(f, n=3):
        for jp in range(n):
            pre[f].append(load_gu(f, jp))

    def ffn(f, ntt):
        N = ntt * 128
        for jp in range(NJ // 2):
            if pre[f]:
                w, bw = pre[f].pop(0)
            else:
                w, bw = load_gu(f, jp)
            wv = w[:].rearrange("p (j s c f) -> p j s c f", j=2, s=2, c=8)
            for jj in range(2):
                j = 2 * jp + jj
                bg = nxt("a", 2) * 2
                for s in range(2):
                    for c in range(8):
                        mm(psb[bg + s][:, 0:N], wv[:, jj, s, c, :], aT[:, c, 0:N], c == 0, c == 7,
                           (bw, b_aT), (b_ps[bg + s],))
                si = 0
                act(sg[si][:, 0:N], psb[bg][:, 0:N], AF.Silu, (b_ps[bg],), (b_sg[si],))
                tt("dve", hT[:, j, 0:N], sg[si][:, 0:N], psb[bg + 1][:, 0:N], ALU.mult,
                   (b_sg[si], b_ps[bg + 1]), (b_hT,))
        for jg in range((NJ + 3) // 4):
            nj = min(4, NJ - 4 * jg)
            w, bw = wslot()
            dma("sp", w[:, 0:nj * D].rearrange("p (j d) -> p j d", j=nj),
                wd_v[f, 512 * jg:512 * jg + 128 * nj, :].rearrange("(j p) d -> p j d", p=128), (b_scr_w,), (bw,), bw)
            wv = w[:].rearrange("p (j d) -> p j d", j=4)
            for jj in range(nj):
                j = 4 * jg + jj
                for t in range(ntt):
                    for hf in range(2):
                        mm(psb[2 * t + hf][:, :], hT[:, j, t * 128:(t + 1) * 128], wv[:, jj, hf * 512:(hf + 1) * 512],
                           j == 0, j == NJ - 1, (bw, b_hT), (b_ps[2 * t + hf],))
        for t in range(ntt):
            for hf in range(2):
                sl = res[:, t, hf * 512:(hf + 1) * 512]
                stt(sl, psb[2 * t + hf][:, :], C_FFN, sl, ALU.mult, ALU.add, (b_ps[2 * t + hf], b_res[t]), (b_res[t],))

    def layer_norm(l, ntt):
        dma("sp", lng[:], lnp[l, 0], (), (b_lng,), b_lng)
        dma("sp", lnb[:], lnp[l, 1], (), (b_lnb,), b_lnb)
        L0 = 0
        for t in range(ntt):
            for hf in range(2):
                P.op("dve", lambda e, t=t, hf=hf: e.bn_stats(out=bst[:, t, hf, :], in_=res[:, t, hf * 512:(hf + 1) * 512]),
                     (b_res[t],), (b_ln[t],))
        for t in range(ntt):
            P.op("dve", lambda e, t=t: e.bn_aggr(out=lns[:, t, 0:2], in_=bst[:, t, :, :].rearrange("p a b -> p (a b)")),
                 (b_ln[t],), (b_ln[t],))
        for t in range(ntt):
            act(lns[:, t, 2:3], lns[:, t, 1:2], AF.Ln, (b_ln[t], b_c), (b_ln[t],), bias=cons[:, 1:2])
        for t in range(ntt):
            act(lns[:, t, 3:4], lns[:, t, 2:3], AF.Exp, (b_ln[t],), (b_ln[t],), scale=-0.5)
        for t in range(ntt):
            ts("dve", lns[:, t, 4:5], lns[:, t, 0:1], -1.0, lns[:, t, 3:4], ALU.mult, ALU.mult, (b_ln[t],), (b_ln[t],))
        for t in range(ntt):
            act(res[:, t, :], res[:, t, :], AF.Identity, (b_res[t], b_ln[t]), (b_res[t],), bias=lns[:, t, 4:5], scale=lns[:, t, 3:4])
        for t in range(ntt):
            tt("pool", res[:, t, :], res[:, t, :], lng[:], ALU.mult, (b_res[t], b_lng), (b_res[t],))
            tt("dve", res[:, t, :], res[:, t, :], lnb[:], ALU.add, (b_res[t], b_lnb), (b_res[t],))

    def rope_fix(src, dst, H, T, r, w):
        if cfg.norope == 1:
            return
        sv = src.rearrange("p (h d) -> p h d", h=H)
        dv = dst.rearrange("p (h d) -> p h d", h=H)
        cosb = rope[:, T, 0:8].unsqueeze(1).broadcast_to([128, H, 8])
        sinb = rope[:, T, 8:16].unsqueeze(1).broadcast_to([128, H, 8])
        tv = rtmp[:].rearrange("p a (h d) -> p a h d", h=8)
        t1, t2, t3, t4 = (tv[:, i, 0:H, :] for i in range(4))
        xv = xr[:, 0:H, :]
        act(xv, sv[:, :, 0:16], AF.Copy, r, (b_xr,))
        if cfg.norope == 2:
            return
        rr = (b_xr, b_c)
        tt("pool", t1, xv[:, :, 0:8], cosb, ALU.mult, rr, (b_rtmp,))
        tt("pool", t2, xv[:, :, 8:16], sinb, ALU.mult, rr, (b_rtmp,))
        tt("pool", t3, xv[:, :, 8:16], cosb, ALU.mult, rr, (b_rtmp,))
        tt("pool", t4, xv[:, :, 0:8], sinb, ALU.mult, rr, (b_rtmp,))
        if cfg.norope == 3:
            return
        fe = "dve"
        tt(fe, dv[:, :, 0:8], t1, t2, ALU.subtract, (b_rtmp,), w)
        tt(fe, dv[:, :, 8:16], t3, t4, ALU.add, (b_rtmp,), w)

    zsel = {"kb": 0, "vb": 0, "z5": 0}

    def znext(kind):
        zsel[kind] = (zsel[kind] + 1) % 2
        return zsel[kind]

    def ingest_kb(T, q="pool", e0="act", i2=None):
        if i2 is None:
            i2 = zsel["kb"]
        zkb, b_zkb, ktmp, b_ktmp = zkb2[i2], b_zkb2[i2], ktmp2[i2], b_ktmp2[i2]
        bk = 4 + nxt("t", 4)
        for c in range(4):
            tr(psb[bk][:, c * 128:(c + 1) * 128], zkb[:, c * 128:(c + 1) * 128], identf[:], (b_zkb, b_c), (b_ps[bk],))
        kv = ktmp[:].rearrange("r (c two) k -> r c two k", two=2)
        cp(e0, kv[0:64, :, 0, :], psb[bk][0:64, :].rearrange("p (c k) -> p c k", c=4), (b_ps[bk],), (b_ktmp,))
        cp("dve", kv[0:64, :, 1, :], psb[bk][64:128, :].rearrange("p (c k) -> p c k", c=4), (b_ps[bk],), (b_ktmp,))
        dma(q, kTb_s[:, :, T * 128:(T + 1) * 128].rearrange("h r c -> r h c"), ktmp[:], (b_ktmp,), (b_ks[T // 8],), b_ktmp)

    def ingest_vb(T, q="pool", ce="act", i2=None):
        if i2 is None:
            i2 = zsel["vb"]
        zvb, b_zvb, stv, b_stv = zvb2[i2], b_zvb2[i2], stv2[i2], b_stv2[i2]
        cp(ce, stv[:, :, 0:64], zvb[:].rearrange("p (h d) -> p h d", h=8), (b_zvb,), (b_stv,))
        dma(q, vb_s[:, :, T, :].rearrange("h p c -> p h c"), stv[:], (b_stv,), (b_vs[T // 8],), b_stv)

    def ingest_5(T, i2=None):
        if i2 is None:
            i2 = zsel["z5"]
        z5, b_z5 = z52[i2], b_z52[i2]
        if cfg.g5 & 32:
            bk = 4 + nxt("t", 4)
            tr(psb[bk][:, 0:128], z5[:, KA:KA + 128], identf[:], (b_z5, b_c), (b_ps[bk],))
            tr(psb[bk][:, 128:256], z5[:, KI:KI + 128], identf[:], (b_z5, b_c), (b_ps[bk],))
            cp("act", kTa[:, T * 128:(T + 1) * 128], psb[bk][:, 0:128], (b_ps[bk],), (b_kTa,))
            cp("dve", kiT[:, T * 128:(T + 1) * 128], psb[bk][:, 128:256], (b_ps[bk],), (b_kiT,))
        if cfg.g5 & 64:
            cp("dve", vpa[:, T, 0:64], z5[:, VA:VA + 64], (b_z5,), (b_vpa,))
            cp("dve", vpa[:, T, 128:192], z5[:, VA + 64:VA + 128], (b_z5,), (b_vpa,))
        if cfg.g5 & 128:
            bk2 = 4 + nxt("t", 4)
            mm(psb[bk2][:, 0:8], Utri[:], z5[:, LF:LF + 8], True, T == 0, (b_z5, b_c), (b_ps[bk2],))
            if T > 0:
                mm(psb[bk2][:, 0:8], Elast[:], cum[:, T - 1, :], False, True, (b_cum, b_c), (b_ps[bk2],))
            cp("dve", cum[:, T, :], psb[bk2][:, 0:8], (b_ps[bk2],), (b_cum,))

    def g5_pass2(t, T, i2, outs):
        z5, b_z5 = z52[i2], b_z52[i2]
        w_ = wf[i2]
        act(aab[:, t, :], w_[:, 0:8], AF.Abs, (b_wf[i2],), (b_aab,), scale=KAPPA)
        act(sgn[:, t, :], w_[:, 0:8], AF.Sign, (b_wf[i2],), (b_sgn,))
        tt("dve", sm[:, 8:16], w_[:, 8:16], bfb[:], ALU.add, (b_wf[i2], b_c), (b_sm2,))
        act(sm[:, 16:24], sm[:, 8:16], AF.Exp, (b_sm2,), (b_sm2,), scale=-1.0)
        act(sm[:, 24:32], sm[:, 16:24], AF.Ln, (b_sm2, b_c), (b_sm2,), bias=cons[:, 0:1])
        ts("dve", z5[:, LF:LF + 8], sm[:, 24:32], -1.0, None, ALU.mult, None, (b_sm2,), (b_z5,))
        if outs is not None:
            n = outs["rows"]
            dma("pool", outs["ka"][t], z5[0:n, KA:KA + 128], (b_z5,), (b_out,), b_z5)
            dma("pool", outs["va"][t], z5[0:n, VA:VA + 128], (b_z5,), (b_out,), b_z5)
            dma("pool", outs["ki"][t], z5[0:n, KI:KI + 64], (b_z5,), (b_out,), b_z5)
            dma("pool", outs["lf"][t], z5[0:n, LF:LF + 8], (b_z5,), (b_out,), b_z5)
        if outs is None or not outs.get("defer"):
            ingest_5(T, i2)
        else:
            dma("pool", zs_5[t], z5[:], (b_z5,), (b_zs[t],), b_z5)

    def project(ntt, Ts, outs):
        goffs = [0, 512, 1024, 1536, 2048, 2560]
        gw = [512, 512, 512, 512, 512, 336]
        pending = []

        def defer(fn):
            pending.append(fn)
            while len(pending) > 1:
                pending.pop(0)()

        for g in cfg.pg:
            w, bw = wslot()
            wv = w[:, 0:8 * gw[g]].rearrange("p (c x) -> p c x", c=8)
            dma("sp", wv, win_v[:, :, goffs[g]:goffs[g] + gw[g]], (b_scr_w,), (bw,), bw)
            for t in range(ntt):
                T = Ts[t]
                bk = nxt("a", 4)
                pw = psb[bk][:, 0:gw[g]]
                for c in range(8):
                    mm(pw, aT[:, c, t * 128:(t + 1) * 128], wv[:, c, :], c == 0, c == 7, (bw, b_aT), (b_ps[bk],))
                bp = b_ps[bk]
                if g in (0, 1):
                    si = nxt("stq", 2)
                    cp("act", stq[si][:], pw, (bp,), (b_stq[si],))
                    rope_fix(pw, stq[si][:], 8, T, (bp,), (b_stq[si],))
                    def fin(si=si, g=g, t=t):
                        bk2 = 4 + nxt("t", 4)
                        pv = psbf(bk2)
                        for c in range(4):
                            tr(pv[:, c * 128:(c + 1) * 128], stq[si][:, c * 128:(c + 1) * 128], identb[:],
                               (b_stq[si], b_c), (b_ps[bk2],))
                        dst, bd = (qTa, b_qTa) if g == 0 else (qiT, b_qiT)
                        cp("dve", dst[:, :, t * 128:(t + 1) * 128], pv[:, 0:512].rearrange("p (c x) -> p c x", c=4),
                           (b_ps[bk2],), (bd,))
                    defer(fin)
                elif g == 2:
                    cp("act", stb[:, t, :, 0:64], pw.rearrange("p (h d) -> p h d", h=8), (bp,), (b_stb[t],))
                elif g == 3:
                    i2 = znext("kb"); zkb, b_zkb = zkb2[i2], b_zkb2[i2]
                    cp("act", zkb[:], pw, (bp,), (b_zkb,))
                    if outs is not None:
                        dma("pool", outs["kb"][t], zkb[0:outs["rows"], :], (b_zkb,), (b_out,), b_zkb)
                    if outs is None or not outs.get("defer"):
                        defer(lambda T=T, i2=i2: ingest_kb(T, e0="dve", i2=i2))
                    else:
                        dma("pool", zs_kb[t], zkb[:], (b_zkb,), (b_zs[t],), b_zkb)
                elif g == 4:
                    i2 = znext("vb"); zvb, b_zvb = zvb2[i2], b_zvb2[i2]
                    cp("act", zvb[:], pw, (bp,), (b_zvb,))
                    if outs is not None:
                        dma("pool", outs["vb"][t], zvb[0:outs["rows"], :], (b_zvb,), (b_out,), b_zvb)
                    if outs is None or not outs.get("defer"):
                        ingest_vb(T, ce="pool")
                    else:
                        dma("pool", zs_vb[t], zvb[:], (b_zvb,), (b_zs[t],), b_zvb)
                else:
                    i2 = znext("z5"); z5, b_z5 = z52[i2], b_z52[i2]
                    cp("act", z5[:, 0:320], pw[:, 0:320], (bp,), (b_z5,))
                    cp("act", wf[i2][:], pw[:, 320:336], (bp,), (b_wf[i2],))
                    rope_fix(pw[:, 0:128], z5[:, KA:KA + 128], 2, T, (bp,), (b_z5,))
                    rope_fix(pw[:, 256:320], z5[:, KI:KI + 64], 1, T, (bp,), (b_z5,))
                    cp("dve", z5[:, KI + 64:KI + 128], z5[:, KI:KI + 64], (b_z5,), (b_z5,))
                    if t > 0:
                        g5_pass2(t - 1, Ts[t - 1], 1 - i2, outs)
                    if t == ntt - 1:
                        g5_pass2(t, T, i2, outs)
        while pending:
            pending.pop(0)()

    def fox_prep(T0, nT, tts, Ts):
        bk = 4 + nxt("t", 4)
        mm(psb[bk][:, 0:8], onesf[:], cum[:, T0, :], True, True, (b_cum, b_c), (b_ps[bk],))
        act(cbc[:], psb[bk][:, 0:8], AF.Copy, (b_ps[bk],), (b_cbc,), scale=1.0 / 128.0)
        tt("dve", biasb[:, 0:nT, :], cbc[:].unsqueeze(1).broadcast_to([128, nT, 8]), cum[:, 0:nT, :], ALU.subtract,
           (b_cbc, b_cum), (b_biasb,))
        for t, T in zip(tts, Ts):
            tt("dve", sm[:, 32:40], cum[:, T, :], cbc[:], ALU.subtract, (b_cum, b_cbc), (b_sm,))
            act(stb[:, t, :, 64:65], sm[:, 32:40].unsqueeze(2), AF.Copy, (b_sm,), (b_stb[t],), scale=8.0)
            bk = 4 + nxt("t", 4)
            pv = psbf(bk)
            for h in range(8):
                tr(pv[0:65, h * 128:(h + 1) * 128], stb[:, t, h, :], identb[:], (b_stb[t], b_c), (b_ps[bk],))
            cp("dve", qTb[:, :, t * 128:(t + 1) * 128], pv[0:65, 0:1024].rearrange("r (h c) -> r h c", h=8),
               (b_ps[bk],), (b_qTb,))

    LA = 3
    QB = [0, 1, 2, 6, 7]

    def pipeline(n, A, C):
        for i in range(n + LA):
            if i < n:
                A(i)
            if i >= LA:
                C(i - LA)

    def prep_idx(t):
        P.op("pool", lambda e: e.tensor_tensor(out=Dm[:], in0=identb[:].unsqueeze(1).broadcast_to([128, 8, 128]),
                                               in1=sgn[:, t, :].unsqueeze(2).broadcast_to([128, 8, 128]), op=ALU.mult),
             (b_c, b_sgn), (b_Dm,))
        qv = qip[0][:].rearrange("p (c two) q -> p c two q", two=2)
        cp("pool", qv[0:64, :, 0, :], qiT[0:64, :, t * 128:(t + 1) * 128], (b_qiT,), (b_qip[0],))
        cp("pool", qv[64:128, :, 1, :], qiT[64:128, :, t * 128:(t + 1) * 128], (b_qiT,), (b_qip[0],))

    def prep_dsa(t):
        cp("pool", qap[0][0:64, 0:4, :], qTa[0:64, :, t * 128:(t + 1) * 128], (b_qTa,), (b_qap[0],))
        cp("pool", qap[0][64:128, 4:8, :], qTa[64:128, :, t * 128:(t + 1) * 128], (b_qTa,), (b_qap[0],))

    def indexer_scores(t, T, sample):
        S = 128 * (T + 1)
        qi_ = 0
        items = [(c0, h) for c0 in range(0, S, 512) for h in range(8)]
        st = {}

        def A(i):
            c0, h = items[i]
            wd = min(512, S - c0)
            bk = QB[nxt("q", 5)]
            mm(psb[bk][:, 0:wd], qip[qi_][:, h, :], kiT[:, c0:c0 + wd], True, True, (b_qip[qi_], b_kiT), (b_ps[bk],))
            ri = nxt("rh", 4)
            act(rh[ri][:, 0:wd], psb[bk][:, 0:wd], AF.Relu, (b_ps[bk], b_aab), (b_rh[ri],), scale=aab[:, t, h:h + 1])
            st[i] = ri

        def C(i):
            c0, h = items[i]
            wd = min(512, S - c0)
            ri = st.pop(i)
            mm(psb[3][:, 0:wd], Dm[:, h, :], rh[ri][:, 0:wd], h == 0, h == 7, (b_Dm, b_rh[ri]), (b_ps[3],))
            if h == 7:
                cp("dve", score[:, c0:c0 + wd], psb[3][:, 0:wd], (b_ps[3],), (b_score,))

        pipeline(len(items), A, C)

    def bisect(t, T, sample):
        S = 128 * (T + 1)
        sc = score[:, 0:S]
        P.op("dve", lambda e: e.tensor_reduce(out=sm[:, 40:41], in_=sc, axis=AX.X, op=ALU.max), (b_score,), (b_sm,))
        P.op("dve", lambda e: e.tensor_reduce(out=sm[:, 41:42], in_=sc, axis=AX.X, op=ALU.min), (b_score,), (b_sm,))
        ts("dve", sm[:, 42:43], sm[:, 40:41], sm[:, 41:42], 1.001, ALU.subtract, ALU.mult, (b_sm,), (b_sm,))
        ts("dve", sm[:, 42:43], sm[:, 42:43], 2e-3, None, ALU.add, None, (b_sm,), (b_sm,))
        ts("dve", sm[:, 43:44], sm[:, 41:42], -1e-3, None, ALU.add, None, (b_sm,), (b_sm,))
        ts("dve", wtab[:], pow2[:], sm[:, 42:43], None, ALU.mult, None, (b_sm, b_c), (b_sm,))
        tt("dve", score[:, S - 128:S], score[:, S - 128:S], (adms if sample else admp)[:], ALU.add, (b_score, b_c), (b_score,))
        for k in range(NIT):
            tt("dve", sm[:, 44:45], sm[:, 43:44], wtab[:, k:k + 1], ALU.add, (b_sm,), (b_sm,))
            ts("dve", junk8[:, 0:S], sc, sm[:, 44:45], None, ALU.is_ge, ALU.add, (b_score, b_sm), (b_junk, b_sm),
               accum=sm[:, 45:46])
            ts("dve", sm[:, 46:47], sm[:, 45:46], 255.5, wtab[:, k:k + 1], ALU.is_ge, ALU.mult, (b_sm,), (b_sm,))
            tt("dve", sm[:, 43:44], sm[:, 43:44], sm[:, 46:47], ALU.add, (b_sm,), (b_sm,))
        cp("dve", sm[:, 48 + (t % 4):49 + (t % 4)], sm[:, 43:44], (b_sm,), (b_thr,))

    def bisect_final(t, T):
        S = 128 * (T + 1)
        ts("dve", mask[:, 0:S], score[:, 0:S], sm[:, 48 + (t % 4):49 + (t % 4)], NEG, ALU.is_lt, ALU.mult,
           (b_score, b_thr), (b_mask,))

    def normalize(bk, orow, drow, c0n, c1n, dsts):
        ni = 0
        act(osb[ni][:, c0n:c1n], psb[bk][orow:orow + 64, c0n:c1n], AF.Copy, (b_ps[bk],), (b_osb[ni],))
        act(rdl[ni][:, c0n:c1n], psb[bk][drow:drow + 64, c0n:c1n], AF.Ln, (b_ps[bk],), (b_rdl[ni],))
        act(rdl[ni][:, c0n:c1n], rdl[ni][:, c0n:c1n], AF.Exp, (b_rdl[ni],), (b_rdl[ni],), scale=-1.0)
        for (c0, c1, dst) in dsts:
            tt("pool", dst, osb[ni][:, c0:c1], rdl[ni][:, c0:c1], ALU.mult, (b_osb[ni], b_rdl[ni]), (b_mixT,))

    def dsa(t, T):
        qa_ = 0
        for g in range(2):
            bo = 4 + (g % 2)
            st = {}

            def A(k):
                bk = QB[nxt("q", 5)]
                mm(psb[bk][:, :], kTa[:, k * 128:(k + 1) * 128], qap[qa_][:, 4 * g:4 * g + 4, :],
                   True, False, (b_kTa, b_qap[qa_]), (b_ps[bk],))
                mm(psb[bk][:, :], mask[:, k * 128:(k + 1) * 128], ident4[:], False, True, (b_mask, b_c), (b_ps[bk],))
                pi = nxt("pt", 4)
                act(PT[pi][:], psb[bk][:, :], AF.Exp, (b_ps[bk],), (b_PT[pi],), scale=0.125)
                st[k] = pi

            def C(k):
                pi = st.pop(k)
                lhs = vpa[:, k, 0:128] if g == 0 else vpa[:, k, 64:192]
                mm(psb[bo][:, :], lhs, PT[pi][:], k == 0, k == T, (b_vpa, b_PT[pi]), (b_ps[bo],))

            pipeline(T + 1, A, C)
            orow, drow = (0, 64) if g == 0 else (64, 0)
            dsts = []
            for j in range(4):
                dsts.append((j * 128, (j + 1) * 128, mixT[64 * (j % 2):64 * (j % 2) + 64, 2 * g + j // 2, t * 128:(t + 1) * 128]))
            normalize(bo, orow, drow, 0, 512, dsts)

    def fox(t0, nq, T0, heads):
        nT = T0 + nq
        Q0 = t0 * 128
        NQ = nq * 128
        for h in heads:
            bo = 4 + (h % 2)
            chunks = {}
            st = {}

            def load(cidx):
                nt = min(8, nT - 8 * cidx)
                ki = nxt("kc", 3)
                dma("sp", kch[ki][:, 0:nt * 128], kTb_s[h, :, cidx * 1024:cidx * 1024 + nt * 128], (b_ks[cidx],),
                    (b_kch[ki],), b_kch[ki])
                vi = nxt("vc", 3)
                dma("sp", vch[vi][:, 0:nt, :], vb_s[h, :, 8 * cidx:8 * cidx + nt, :], (b_vs[cidx],), (b_vch[vi],), b_vch[vi])
                chunks[cidx] = (ki, vi)

            def A(s):
                cidx, ss = s // 8, s % 8
                if cidx not in chunks:
                    load(cidx)
                ki, vi = chunks[cidx]
                kk = max(0, s - T0)
                c0, c1 = kk * 128, NQ
                bk = QB[nxt("q", 5)]
                diag = s >= T0
                mm(psb[bk][:, c0:c1], kch[ki][0:65, ss * 128:(ss + 1) * 128], qTb[0:65, h, Q0 + c0:Q0 + c1],
                   True, not diag, (b_kch[ki], b_qTb), (b_ps[bk],))
                if diag:
                    mm(psb[bk][:, c0:c0 + 128], identb[:], trib[:], False, True, (b_c,), (b_ps[bk],))
                pi = nxt("pt", 4)
                act(PT[pi][:, c0:c1], psb[bk][:, c0:c1], AF.Exp, (b_ps[bk], b_biasb), (b_PT[pi],),
                    bias=biasb[:, s, h:h + 1], scale=0.125)
                st[s] = (pi, vi, ss, c0, c1)

            def C(s):
                pi, vi, ss, c0, c1 = st.pop(s)
                mm(psb[bo][:, c0:c1], vch[vi][:, ss, :], PT[pi][:, c0:c1], s == 0, s == nT - 1,
                   (b_vch[vi], b_PT[pi]), (b_ps[bo],))

            pipeline(nT, A, C)
            dst = mixT[64 * (h % 2):64 * (h % 2) + 64, 4 + h // 2, Q0:Q0 + NQ]
            normalize(bo, 0, 64, 0, NQ, [(0, NQ, dst)])

    def out_proj(ntt):
        for hf in range(2):
            w, bw = wslot()
            wv = w[:].rearrange("p (c x) -> p c x", c=8)
            dma("sp", wv, wout_v[:, :, hf * 512:(hf + 1) * 512], (b_scr_w,), (bw,), bw)
            for t in range(ntt):
                bk = nxt("a", 4)
                for c in range(8):
                    mm(psb[bk][:, :], mixT[:, c, t * 128:(t + 1) * 128], wv[:, c, :], c == 0, c == 7, (bw, b_mixT), (b_ps[bk],))
                sl = res[:, t, hf * 512:(hf + 1) * 512]
                stt(sl, psb[bk][:, :], C_OUT, sl, ALU.mult, ALU.add, (b_ps[bk], b_res[t]), (b_res[t],))

    def block_front(ntt, Ts, outs):
        if cfg.cut >= 1:
            transpose_res(ntt)
        P.fence([b_score, b_mask], [b_hT])
        if cfg.cut >= 2:
            ffn(0, ntt)
        if cfg.cut >= 3:
            layer_norm(0, ntt)
        if cfg.cut >= 4:
            transpose_res(ntt)
        P.fence([b_hT], [b_score, b_mask])
        if cfg.cut >= 5:
            project(ntt, Ts, outs)

    def block_back(ntt, more):
        out_proj(ntt)
        layer_norm(1, ntt)
        transpose_res(ntt)
        P.fence([b_score, b_mask], [b_hT])
        ffn(1, ntt)
        if more:
            prefetch_ffn(0)
        layer_norm(2, ntt)
        P.fence([b_hT], [b_score, b_mask])

    for b in range(cfg.nseq if cfg.stage >= 1 else 0):
        for i in range(cfg.nblk):
            r0 = 512 * i
            for t4 in range(4):
                dma("sp", res[:, t4, :], xp[b, r0 + 128 * t4:r0 + 128 * (t4 + 1), :], (), (b_res[t4],), b_res[t4])
            Ts = [4 * i + t for t in range(4)]
            rows = lambda o: [o[b, r0 + 128 * t:r0 + 128 * (t + 1), :] for t in range(4)]
            outs = {"rows": 128, "ka": rows(o_ka_p), "va": rows(o_va_p), "ki": rows(o_ki_p), "kb": rows(o_kb_p),
                    "vb": rows(o_vb_p), "lf": rows(o_lf_p)}
            block_front(4, Ts, outs)
            if cfg.debug and b == 0 and i == 0:
                dma("pool", dbg_h1.rearrange("t p d -> p t d"), res[:], tuple(b_res), (b_out,), b_res[1])
            if cfg.stage >= 2:
                fox_prep(4 * i, 4 * i + 4, range(4), Ts)
                prep_idx(0)
                indexer_scores(0, 4 * i, False)
                prep_idx(1)
                bisect(0, 4 * i, False)
                prep_dsa(0)
                for t in range(4):
                    fox(0, 4, 4 * i, [2 * t, 2 * t + 1])
                    bisect_final(t, 4 * i + t)
                    if t + 1 < 4:
                        indexer_scores(t + 1, 4 * i + t + 1, False)
                        if t + 2 < 4:
                            prep_idx(t + 2)
                        bisect(t + 1, 4 * i + t + 1, False)
                    dsa(t, 4 * i + t)
                    if t + 1 < 4:
                        prep_dsa(t + 1)
                if cfg.debug and b == 0 and i == cfg.nblk - 1:
                    dma("pool", dbg_mix.rearrange("c p x -> p c x"), mixT[:], (b_mixT,), (b_out,), b_mixT)
            if cfg.stage >= 9:
                last = (b == cfg.nseq - 1 and i == cfg.nblk - 1)
                block_back(4, (not last) or bool(cfg.sample))
                for t4 in range(4):
                    dma("pool", y_p[b, r0 + 128 * t4:r0 + 128 * (t4 + 1), :], res[:, t4, :], (b_res[t4],), (b_out,), b_res[t4])

    if cfg.sample:
        dma("sp", res[:], xs.rearrange("t p d -> p t d"), (), tuple(b_res), b_res[0])
        srow = lambda o: [o[t] for t in range(4)]
        outs = {"rows": 64, "defer": True, "ka": srow(o_ka_s), "va": srow(o_va_s), "ki": srow(o_ki_s),
                "kb": srow(o_kb_s), "vb": srow(o_vb_s), "lf": srow(o_lf_s)}
        block_front(4, [32] * 4, outs)
        if cfg.stage >= 2:
            for r in range(4):
                for T in range(33):
                    k0 = 128 * T
                    i2 = znext("kb"); zkb, b_zkb = zkb2[i2], b_zkb2[i2]
                    if T < 32:
                        dma("sp", zkb[:], c_kb[r, k0:k0 + 128, :], (), (b_zkb,), b_zkb)
                    else:
                        dma("sp", zkb[:], zs_kb[r], (b_zs[r],), (b_zkb,), b_zkb)
                    ingest_kb(T)
                    i2 = znext("vb"); zvb, b_zvb = zvb2[i2], b_zvb2[i2]
                    if T < 32:
                        dma("sp", zvb[:], c_vb[r, k0:k0 + 128, :], (), (b_zvb,), b_zvb)
                    else:
                        dma("sp", zvb[:], zs_vb[r], (b_zs[r],), (b_zvb,), b_zvb)
                    ingest_vb(T)
                    i2 = znext("z5"); z5, b_z5 = z52[i2], b_z52[i2]
                    if T < 32:
                        dma("sp", z5[:], c_5[r, k0:k0 + 128, :], (), (b_z5,), b_z5)
                    else:
                        dma("sp", z5[:], zs_5[r], (b_zs[r],), (b_z5,), b_z5)
                    ingest_5(T)
                fox_prep(32, 33, [r], [32])
                prep_idx(r)
                indexer_scores(r, 32, True)
                prep_dsa(r)
                bisect(r, 32, True)
                fox(r, 1, 32, list(range(8)))
                bisect_final(r, 32)
                dsa(r, 32)
        if cfg.stage >= 9:
            block_back(4, False)
            dma("pool", y_s.rearrange("t p d -> p t d"), res[0:64, :, :], tuple(b_res), (b_out,), b_res[1])

    P.emit(final_queue="sp")
    stack.close()
    return nc, {e: len(s) for e, s in P.streams.items()}


_PROG = {}


def _get_prog():
    cfg = Cfg()
    key = (cfg.nblk, cfg.nseq, cfg.sample, cfg.stage, cfg.debug, cfg.cut, tuple(cfg.pg), cfg.norope, cfg.g5)
    if key not in _PROG:
        _PROG[key] = (build_program(cfg), cfg)
    return _PROG[key]


def _rope_table():
    half = 8
    inv = (np.float32(500000.0) ** (-np.arange(half, dtype=np.float32) * np.float32(2.0) / np.float32(16.0))).astype(np.float32)
    pos = np.arange(SKEY, dtype=np.float32)
    ang = (pos[:, None] * inv[None, :]).astype(np.float32)
    return np.concatenate([np.cos(ang), np.sin(ang)], axis=1).astype(np.float32)


def kernel(x_prompt, x_sample, cache_k_a, cache_v_a, cache_kidx_a, cache_k_b, cache_v_b, cache_logf_b,
           w_in, b_f, w_out, ln1_g, ln1_b, ffn1_w_gate, ffn1_w_up, ffn1_w_down,
           ln2_g, ln2_b, ln3_g, ln3_b, ffn2_w_gate, ffn2_w_up, ffn2_w_down):
    (nc, _), cfg = _get_prog()
    f32 = np.float32
    A = lambda a: np.ascontiguousarray(np.asarray(a, dtype=f32))
    def gu(wg, wu):
        g = A(wg)[0].reshape(8, 128, NJ, 128)
        u = A(wu)[0].reshape(8, 128, NJ, 128)
        s = np.stack([g, u], axis=0)
        return s.transpose(3, 2, 0, 1, 4)
    wgu = np.stack([gu(ffn1_w_gate, ffn1_w_up), gu(ffn2_w_gate, ffn2_w_up)], axis=0).reshape(2, NJ, 128, 2048)
    wd = np.stack([A(ffn1_w_down)[0], A(ffn2_w_down)[0]], axis=0)
    win = A(w_in)[0][:, _win_perm()].reshape(8, 128, NIN).transpose(1, 0, 2)
    wout = A(w_out)[0].reshape(8, 128, D).transpose(1, 0, 2)
    lnp = np.stack([np.stack([A(g)[0], A(bb)[0]]) for g, bb in ((ln1_g, ln1_b), (ln2_g, ln2_b), (ln3_g, ln3_b))])
    lnp = np.ascontiguousarray(np.broadcast_to(lnp[:, :, None, :], (3, 2, 128, D)))
    bfh = np.ascontiguousarray(np.broadcast_to(A(b_f)[0][None, :], (128, 8)))
    ropeh = _rope_table()
    shared = {"wgu_h": np.ascontiguousarray(wgu), "wd_h": np.ascontiguousarray(wd), "win_h": np.ascontiguousarray(win),
              "wout_h": np.ascontiguousarray(wout), "lnp": lnp, "bf_h": bfh, "rope_h": ropeh}
    xp = A(x_prompt)
    xsm = A(x_sample)
    in_maps = []
    for c in range(N_CORES):
        xs_pad = np.zeros((4, 128, D), f32)
        xs_pad[:, 0:64, :] = xsm[4 * c:4 * c + 4]
        m = dict(shared)
        m.update({
            "xp": np.ascontiguousarray(xp[2 * c:2 * c + 2]),
            "xs": xs_pad,
            "c_5": np.ascontiguousarray(np.concatenate([
                A(cache_k_a)[0, 4 * c:4 * c + 4].reshape(4, SEQ, 128), A(cache_v_a)[0, 4 * c:4 * c + 4].reshape(4, SEQ, 128),
                A(cache_kidx_a)[0, 4 * c:4 * c + 4].reshape(4, SEQ, 64), A(cache_kidx_a)[0, 4 * c:4 * c + 4].reshape(4, SEQ, 64),
                A(cache_logf_b)[0, 4 * c:4 * c + 4].reshape(4, SEQ, 8)], axis=2)),
            "c_kb": A(cache_k_b)[0, 4 * c:4 * c + 4].reshape(4, SEQ, 512),
            "c_vb": A(cache_v_b)[0, 4 * c:4 * c + 4].reshape(4, SEQ, 512),
        })
        in_maps.append(m)
    if cfg.cores < N_CORES:
        res = run_bass_kernel_spmd(nc, in_maps[:cfg.cores], core_ids=list(range(cfg.cores)))
        R = list(res.results) + [res.results[0]] * (N_CORES - cfg.cores)
    else:
        res = run_bass_kernel_spmd(nc, in_maps, core_ids=list(range(N_CORES)))
        R = res.results
    cat = lambda k: np.concatenate([np.asarray(r[k]) for r in R], axis=0)
    y_prompt = cat("y_p")
    y_sample = cat("y_s")
    outs = [y_prompt, y_sample,
            cat("ka_p").reshape(1, 16, SEQ, 2, 64), cat("va_p").reshape(1, 16, SEQ, 2, 64),
            cat("ki_p").reshape(1, 16, SEQ, 64), cat("kb_p").reshape(1, 16, SEQ, 8, 64),
            cat("vb_p").reshape(1, 16, SEQ, 8, 64), cat("lf_p").reshape(1, 16, SEQ, 8),
            cat("ka_s").reshape(1, 32, 64, 2, 64), cat("va_s").reshape(1, 32, 64, 2, 64),
            cat("ki_s").reshape(1, 32, 64, 64), cat("kb_s").reshape(1, 32, 64, 8, 64),
            cat("vb_s").reshape(1, 32, 64, 8, 64), cat("lf_s").reshape(1, 32, 64, 8)]
    if cfg.debug:
        kernel.debug = {k: [np.asarray(r[k]) for r in R] for k in ("dbg_mix", "dbg_h1")}
    return tuple(np.ascontiguousarray(o.astype(f32)) for o in outs)
```
